# Optimizing a Trainium2 kernel written in Bass

```python
import jax, jax.numpy as jnp
from jax import lax
import numpy as np

D_MODEL = 4096
BATCH = 4
SEQ = 4096
DEPTH = 1

D_RWKV = D_MODEL // 2
RWKV_HEAD = 64
RWKV_HEADS = D_RWKV // RWKV_HEAD
DECAY_RANK = 96
ICLR_RANK = 96
GATE_RANK = 256
D_CONV = D_MODEL // 2
CONV_WIDTH = 31
MEM_LEN = 256
XATTN_HEADS = 4
XATTN_HEAD_DIM = D_MODEL // XATTN_HEADS
D_FF = 4 * D_MODEL
DEEPNORM_ALPHA = float((2 * DEPTH) ** 0.25)
DEEPNORM_BETA = float((8 * DEPTH) ** -0.25)
LN_EPS = 1e-5
GN_EPS = 64e-5

N_RWKV_COLS = 3 * D_RWKV + DECAY_RANK + ICLR_RANK + GATE_RANK
N_CONV_COLS = 2 * D_CONV
N_GATE_COLS = 2 * D_MODEL
D_IN = N_RWKV_COLS + N_CONV_COLS + N_GATE_COLS
RWKV_SPLITS = [D_RWKV, 2 * D_RWKV, 3 * D_RWKV, 3 * D_RWKV + DECAY_RANK, 3 * D_RWKV + DECAY_RANK + ICLR_RANK]

kernel_name = "rwkv7_conformer_gated_hybrid_deepnorm"


def layer_norm(x, g, b, eps=LN_EPS):
    xf = x.astype(jnp.float32)
    mu = jnp.mean(xf, -1, keepdims=True)
    var = jnp.mean(jnp.square(xf - mu), -1, keepdims=True)
    return ((xf - mu) * lax.rsqrt(var + eps) * g.astype(jnp.float32) + b.astype(jnp.float32)).astype(x.dtype)


def token_shift(z, mu):
    prev = jnp.pad(z, ((0, 0), (1, 0), (0, 0)))[:, :-1]
    return z + (prev - z) * mu


def rwkv7_scan(r, decay, k, v, a, b):
    B, S, H, N = r.shape

    def step(state, inp):
        r_t, w_t, k_t, v_t, a_t, b_t = inp
        sa = jnp.einsum('bhvk,bhk->bhv', state, a_t)
        state = (state * w_t[:, :, None, :] + sa[..., None] * b_t[:, :, None, :]
                 + v_t[..., None] * k_t[:, :, None, :])
        return state, jnp.einsum('bhvk,bhk->bhv', state, r_t)

    s0 = jnp.zeros((B, H, N, N), jnp.float32)
    xs = tuple(jnp.swapaxes(t, 0, 1) for t in (r, decay, k, v, a, b))
    _, out = lax.scan(step, s0, xs)
    return jnp.swapaxes(out, 0, 1)


def rwkv7_time_mix(z, shift_mix, w0, w_up, a0, a_up, g_up, k_k, k_a, r_k, gn_g, gn_b):
    B, S, _ = z.shape
    f32 = jnp.float32
    z = token_shift(z, shift_mix)
    r, k, v, dw, da, dg = jnp.split(z, RWKV_SPLITS, axis=-1)
    w = -jax.nn.softplus(-(w0 + jnp.tanh(dw) @ w_up)) - 0.5
    decay = jnp.exp(-jnp.exp(w.astype(f32)))
    a = jax.nn.sigmoid(a0 + da @ a_up)
    g = jax.nn.sigmoid(dg) @ g_up
    heads = lambda t: t.reshape(B, S, RWKV_HEADS, RWKV_HEAD).astype(f32)
    kk = heads(k * k_k)
    kk = kk / jnp.maximum(jnp.sqrt(jnp.sum(kk * kk, -1, keepdims=True)), 1e-12)
    k = k * (1.0 + (a - 1.0) * k_a)
    rh, kh, vh, ah = heads(r), heads(k), heads(v), heads(a)
    o = rwkv7_scan(rh, heads(decay), kh, vh, -kk, kk * ah)
    mu = jnp.mean(o, -1, keepdims=True)
    var = jnp.mean(jnp.square(o - mu), -1, keepdims=True)
    o = ((o - mu) * lax.rsqrt(var + GN_EPS)).reshape(B, S, D_RWKV) * gn_g.astype(f32) + gn_b.astype(f32)
    bonus = jnp.sum(rh * kh * r_k.astype(f32), -1, keepdims=True) * vh
    o = o + bonus.reshape(B, S, D_RWKV)
    return (o * g.astype(f32)).astype(z.dtype)


def conformer_conv(z, conv_w, conv_b, ln_g, ln_b):
    u = z[..., :D_CONV] * jax.nn.sigmoid(z[..., D_CONV:])
    u = lax.conv_general_dilated(u, conv_w, window_strides=(1,), padding=[(CONV_WIDTH - 1, 0)],
                                 dimension_numbers=('NWC', 'WIO', 'NWC'),
                                 feature_group_count=D_CONV) + conv_b
    return jax.nn.silu(layer_norm(u, ln_g, ln_b))


def memory_cross_attention(h, mem_n, wq, wk, wv, wo):
    B, S, _ = h.shape
    M = mem_n.shape[1]
    q = (h @ wq).reshape(B, S, XATTN_HEADS, XATTN_HEAD_DIM)
    k = (mem_n @ wk).reshape(B, M, XATTN_HEADS, XATTN_HEAD_DIM)
    v = (mem_n @ wv).reshape(B, M, XATTN_HEADS, XATTN_HEAD_DIM)
    s = jnp.einsum('bshd,bmhd->bhsm', q, k).astype(jnp.float32) * (XATTN_HEAD_DIM ** -0.5)
    p = jax.nn.softmax(s, axis=-1).astype(v.dtype)
    o = jnp.einsum('bhsm,bmhd->bshd', p, v).reshape(B, S, D_MODEL)
    return o @ wo


def setup_inputs(seed: int = 0) -> dict:
    key = jax.random.key(seed)
    ks = iter(jax.random.split(key, 48))
    L = DEPTH
    nrm = lambda shape, scale: jax.random.normal(next(ks), shape, jnp.float32) * scale
    gain = lambda shape: 1.0 + nrm(shape, 0.02)
    beta = DEEPNORM_BETA
    lin = jnp.linspace(0.0, 1.0, D_RWKV, dtype=jnp.float32)
    w0 = -7.0 + 5.0 * lin ** 0.85 + 0.5
    return {
        "x": nrm((BATCH, SEQ, D_MODEL), 1.0),
        "mem": nrm((BATCH, MEM_LEN, D_MODEL), 1.0),
        "w_in": nrm((L, D_MODEL, D_IN), D_MODEL ** -0.5),
        "rwkv_shift_mix": jax.random.uniform(next(ks), (L, N_RWKV_COLS), jnp.float32),
        "rwkv_w0": w0[None, :] + nrm((L, D_RWKV), 0.05),
        "rwkv_w_up": nrm((L, DECAY_RANK, D_RWKV), 0.1 * DECAY_RANK ** -0.5),
        "rwkv_a0": nrm((L, D_RWKV), 0.1),
        "rwkv_a_up": nrm((L, ICLR_RANK, D_RWKV), 0.1 * ICLR_RANK ** -0.5),
        "rwkv_g_up": nrm((L, GATE_RANK, D_RWKV), GATE_RANK ** -0.5),
        "rwkv_k_k": 0.85 + nrm((L, D_RWKV), 0.05),
        "rwkv_k_a": 1.0 + nrm((L, D_RWKV), 0.05),
        "rwkv_r_k": nrm((L, RWKV_HEADS, RWKV_HEAD), 0.1),
        "rwkv_gn_g": gain((L, D_RWKV)),
        "rwkv_gn_b": nrm((L, D_RWKV), 0.02),
        "conv_w": nrm((L, CONV_WIDTH, 1, D_CONV), CONV_WIDTH ** -0.5),
        "conv_b": nrm((L, D_CONV), 0.02),
        "conv_ln_g": gain((L, D_CONV)),
        "conv_ln_b": nrm((L, D_CONV), 0.02),
        "proj_rwkv": nrm((L, D_RWKV, D_MODEL), beta * D_RWKV ** -0.5),
        "proj_conv": nrm((L, D_CONV, D_MODEL), beta * D_CONV ** -0.5),
        "w_out": nrm((L, D_MODEL, D_MODEL), beta * D_MODEL ** -0.5),
        "ln1_g": gain((L, D_MODEL)),
        "ln1_b": nrm((L, D_MODEL), 0.02),
        "ln_mem_g": gain((D_MODEL,)),
        "ln_mem_b": nrm((D_MODEL,), 0.02),
        "xattn_wq": nrm((L, D_MODEL, D_MODEL), D_MODEL ** -0.5),
        "xattn_wk": nrm((L, D_MODEL, D_MODEL), D_MODEL ** -0.5),
        "xattn_wv": nrm((L, D_MODEL, D_MODEL), beta * D_MODEL ** -0.5),
        "xattn_wo": nrm((L, D_MODEL, D_MODEL), beta * D_MODEL ** -0.5),
        "ln2_g": gain((L, D_MODEL)),
        "ln2_b": nrm((L, D_MODEL), 0.02),
        "mlp_w1": nrm((L, D_MODEL, D_FF), beta * D_MODEL ** -0.5),
        "mlp_w2": nrm((L, D_FF, D_MODEL), beta * D_FF ** -0.5),
        "ln3_g": gain((L, D_MODEL)),
        "ln3_b": nrm((L, D_MODEL), 0.02),
    }


def reference(x, mem, w_in, rwkv_shift_mix, rwkv_w0, rwkv_w_up, rwkv_a0, rwkv_a_up, rwkv_g_up,
              rwkv_k_k, rwkv_k_a, rwkv_r_k, rwkv_gn_g, rwkv_gn_b, conv_w, conv_b, conv_ln_g,
              conv_ln_b, proj_rwkv, proj_conv, w_out, ln1_g, ln1_b, ln_mem_g, ln_mem_b,
              xattn_wq, xattn_wk, xattn_wv, xattn_wo, ln2_g, ln2_b, mlp_w1, mlp_w2, ln3_g, ln3_b):
    alpha = DEEPNORM_ALPHA
    mem_n = layer_norm(mem, ln_mem_g, ln_mem_b)
    h = x
    for l in range(DEPTH):
        z = h @ w_in[l]
        z_rwkv = z[..., :N_RWKV_COLS]
        z_conv = z[..., N_RWKV_COLS:N_RWKV_COLS + N_CONV_COLS]
        z_gate = z[..., N_RWKV_COLS + N_CONV_COLS:]
        o_r = rwkv7_time_mix(z_rwkv, rwkv_shift_mix[l], rwkv_w0[l], rwkv_w_up[l], rwkv_a0[l],
                             rwkv_a_up[l], rwkv_g_up[l], rwkv_k_k[l], rwkv_k_a[l], rwkv_r_k[l],
                             rwkv_gn_g[l], rwkv_gn_b[l])
        o_c = conformer_conv(z_conv, conv_w[l], conv_b[l], conv_ln_g[l], conv_ln_b[l])
        gate_r = jax.nn.sigmoid(z_gate[..., :D_MODEL])
        gate_c = jax.nn.sigmoid(z_gate[..., D_MODEL:])
        merged = gate_r * (o_r @ proj_rwkv[l]) + gate_c * (o_c @ proj_conv[l])
        h = layer_norm(alpha * h + merged @ w_out[l], ln1_g[l], ln1_b[l])
        ca = memory_cross_attention(h, mem_n, xattn_wq[l], xattn_wk[l], xattn_wv[l], xattn_wo[l])
        h = layer_norm(alpha * h + ca, ln2_g[l], ln2_b[l])
        ff = jnp.square(jax.nn.relu(h @ mlp_w1[l])) @ mlp_w2[l]
        h = layer_norm(alpha * h + ff, ln3_g[l], ln3_b[l])
    return h
```

```python
import contextlib
import numpy as np
import concourse.bass as bass
import concourse.mybir as mybir
from concourse.bass_utils import run_bass_kernel_spmd

F32 = mybir.dt.float32
BF16 = mybir.dt.bfloat16
AF = mybir.ActivationFunctionType
ALU = mybir.AluOpType
AX = mybir.AxisListType
DECAY_C = 0.6065306597126334


class Cfg:
    def __init__(s, D=4096, SEQ=4096, BATCH=4, MEM=256, XH=4):
        s.D = D; s.SEQ = SEQ; s.BATCH = BATCH; s.MEM = MEM; s.XH = XH
        s.DR = D // 2; s.NHP = s.DR // 128
        s.RW = 96; s.RA = 96; s.RG = 256
        s.DC = D // 2; s.CW = 31
        s.DFF = 4 * D
        s.T = SEQ // 2; s.HALO = 32; s.TH = s.T + 32
        s.KT = D // 128
        s.NRW = 3 * s.DR + s.RW + s.RA + s.RG
        s.DIN = s.NRW + 2 * s.DC + 2 * D
        s.NJIN = (s.DIN + 127) // 128
        s.C_K = s.DR; s.C_V = 2 * s.DR; s.C_DW = 3 * s.DR; s.C_DA = s.C_DW + 96; s.C_DG = s.C_DA + 96
        s.C_CONV = s.NRW; s.C_GATE = s.NRW + 2 * s.DC
        s.alpha = float(2.0 ** 0.25)
        s.NCORES = 2 * BATCH
        s.SEG = min(512, s.T)
        s.TT = 512
        s.XD = D // XH


class Prog:
    ENG = ("pe", "act", "dve", "pool", "sp")

    def __init__(self, nc, debug=False):
        self.nc = nc
        self.debug = debug
        self.es = contextlib.ExitStack()
        self.eng = dict(pe=nc.tensor, act=nc.scalar, dve=nc.vector, pool=nc.gpsimd, sp=nc.sync)
        self.sems = []
        self.psem = {}
        for e in self.ENG:
            self.psem[e] = self.new_sem("p_" + e)
        self.pcnt = {e: 0 for e in self.ENG}
        self.seen = {e: {} for e in self.ENG}
        self.lastw = {}
        self.lastr = {}
        self.dsem = {}
        self.n_inst = 0
        self.dram_out = {}

    def new_sem(self, name):
        s = self.es.enter_context(self.nc.semaphore(name))
        self.sems.append(s)
        return len(self.sems) - 1

    def sbuf(self, name, shape, dt):
        self._nalloc = getattr(self, "_nalloc", 0) + 1
        st = self.scopes[-1] if getattr(self, "scopes", None) else self.es
        return st.enter_context(self.nc.sbuf_tensor(f"{name}_{self._nalloc}", list(shape), dt))

    @contextlib.contextmanager
    def scope(self):
        if not hasattr(self, "scopes"):
            self.scopes = []
        st = contextlib.ExitStack()
        self.scopes.append(st)
        try:
            yield
        finally:
            self.barrier()
            self.scopes.pop()
            st.close()

    def barrier(self):
        toks = [(self.psem[e], self.pcnt[e], e) for e in self.ENG if self.pcnt[e] > 0]
        toks += [(ent[0], ent[1], "dma") for ent in self.dsem.values()]
        for e in self.ENG:
            self._wait(e, toks)

    def psum(self, name, shape, dt):
        return self.es.enter_context(self.nc.psum_tensor(name, list(shape), dt))

    def dram(self, name, shape, dt):
        kind = "ExternalOutput" if self.debug else "Internal"
        t = self.nc.dram_tensor(name, list(shape), dt, kind=kind)
        return t.ap()

    def _wait(self, e, toks, raw=()):
        need = {}
        for (si, val, te) in toks:
            if te == e:
                continue
            if val > need.get(si, 0):
                need[si] = val
        for (si, val, te) in raw:
            if val > need.get(si, 0):
                need[si] = val
        for si, val in need.items():
            if self.seen[e].get(si, 0) < val:
                self.eng[e].wait_ge(self.sems[si], val)
                self.seen[e][si] = val

    def _raw(self, reads):
        toks = []
        for r in reads:
            toks.extend(self.lastw.get(r, {}).values())
        return toks

    def _deps(self, reads, writes, merge=False):
        toks = []
        for w in writes:
            if not merge:
                toks.extend(self.lastw.get(w, {}).values())
            toks.extend(self.lastr.get(w, {}).values())
        return toks

    def _record(self, tok, reads, writes, merge):
        si = tok[0]
        for w in writes:
            if merge:
                self.lastw.setdefault(w, {})[si] = tok
            else:
                self.lastw[w] = {si: tok}
                self.lastr[w] = {}
        for r in reads:
            self.lastr.setdefault(r, {})[si] = tok

    @staticmethod
    def _is_ps(k):
        return isinstance(k, tuple) and len(k) > 0 and k[0] == "psb"

    def op(self, e, fn, reads=(), writes=(), inc=True, merge=False):
        pk = [k for k in list(reads) + list(writes) if self._is_ps(k)]
        reads = [k for k in reads if not self._is_ps(k)]
        writes = [k for k in writes if not self._is_ps(k)]
        toks = self._deps(reads, writes, merge)
        for k in pk:
            toks.extend(self.lastw.get(k, {}).values())
        self._wait(e, toks, self._raw(reads))
        inst = fn(self.eng[e])
        self.n_inst += 1
        if inc:
            self.pcnt[e] += 1
            inst.then_inc(self.sems[self.psem[e]], 1)
            tok = (self.psem[e], self.pcnt[e], e)
        else:
            tok = (self.psem[e], self.pcnt[e] + 1, e)
        self._record(tok, reads, writes, merge)
        for k in pk:
            self.lastw.setdefault(k, {})[tok[0]] = tok
        return inst

    def dma(self, out, in_, reads=(), writes=(), sem="d0", q="sp", merge=False, **kw):
        self._wait(q, self._deps(reads, writes, merge), self._raw(reads))
        if sem not in self.dsem:
            self.dsem[sem] = [self.new_sem("d_" + sem), 0]
        ent = self.dsem[sem]
        inst = self.eng[q].dma_start(out=out, in_=in_, **kw)
        ent[1] += 16
        inst.then_inc(self.sems[ent[0]], 16)
        self.n_inst += 1
        tok = (ent[0], ent[1], "dma")
        self._record(tok, reads, writes, merge)

    def finish(self, keys):
        toks = []
        for k in keys:
            toks.extend(self.lastw.get(k, {}).values())
        self._wait("sp", toks)


class Ring:
    def __init__(self, name, items):
        self.name = name; self.items = items; self.i = 0

    def next(self):
        k = self.i % len(self.items)
        self.i += 1
        return (self.name, k), self.items[k]


class Builder:
    def __init__(self, cfg, debug=False):
        self.c = cfg
        self.debug = debug
        self.nc = bass.Bass("TRN2", target_bir_lowering=False)
        self.P = Prog(self.nc, debug)
        self.inputs = {}

    def inp(self, name, shape, dt=F32):
        t = self.nc.dram_tensor(name, list(shape), dt, kind="ExternalInput").ap()
        self.inputs[name] = t
        return t

    def setup_consts(self):
        P, c = self.P, self.c
        cin = self.inp("consts", [128, 1792])
        self.cf = P.sbuf("cf", [128, 1792], F32)
        P.dma(self.cf[:, :], cin[:, :], writes=["cf"], sem="ccf")
        cf = self.cf
        self.ident_f = cf[:, 0:128]
        self.bones_f = cf[:, 128:256]
        self.hmask = cf[:, 704:706]
        self.hmaskn = cf[:, 706:708]
        self.reset = cf[:, 708:708 + 1024]
        self.cb = P.sbuf("cb", [128, 704], BF16)
        P.op("dve", lambda e: e.tensor_copy(out=self.cb[:, :], in_=cf[:, 0:704]), reads=["cf"], writes=["cb"])
        cb = self.cb
        self.ident_b = cb[:, 0:128]
        self.mask_sl = cf[:, 256:384]
        self.mask_su_u = cf[:, 384:640]
        self.mask_su = cf[:, 384:512]
        self.isel_b = cb[:, 640:704]
        self.ones_b = P.sbuf("ones_b", [128, 2, 128], BF16)
        P.op("pool", lambda e: e.memset(self.ones_b[:, 0, :], 1.0 / c.D), writes=["ones0"])
        P.op("pool", lambda e: e.memset(self.ones_b[:, 1, :], 1.0 / c.DC), writes=["ones1"])
        self.ps = [P.psum(f"ps{i}", [128, 512], F32) for i in range(8)]
        self.ps_i = 0
        self.psh_i = 0

    def bank(self):
        b = self.ps_i % 8
        self.ps_i += 1
        return b

    def half(self):
        b = self.bank()
        return ("psb", b), self.ps[b][:, 0:256]

    def bank_keys(self, b):
        return [("psb", b)]

    def alloc_ti(self, with_f32=True):
        P, c = self.P, self.c
        self._xin = Ring("xin", [P.sbuf(f"xin{i}", [128, c.D], F32) for i in range(2)])
        if with_f32:
            self._xf = Ring("xf", [P.sbuf(f"xf{i}", [128, c.KT, 128], F32) for i in range(2)])

    def transpose_in(self, src, ntok, xt, xt_key, col0, f32_dst=None, tag="ti"):
        P, c = self.P, self.c
        KT = c.KT
        xin = self._xin
        nblk = (ntok + 127) // 128
        for i in range(nblk):
            nt = min(128, ntok - i * 128)
            xk, xb = xin.next()
            P.dma(xb[0:nt, :], src[i * 128:i * 128 + nt, :], writes=[xk], sem=f"xin{xk[1]}")
            if f32_dst is not None:
                fk, fb = self._xf.next()
            for g in range(0, KT, 4):
                gn = min(4, KT - g)
                b = self.bank()
                bk = self.bank_keys(b)
                for q in range(gn):
                    P.op("pe", lambda e, q=q: e.transpose(self.ps[b][:, q * 128:q * 128 + nt],
                                                          xb[0:nt, (g + q) * 128:(g + q + 1) * 128],
                                                          self.ident_f[0:nt, 0:nt]),
                         reads=[xk, "cf"], writes=bk, inc=(q == gn - 1), merge=(q > 0))
                src_ps = self.ps[b][:, 0:gn * 128].rearrange("p (a t) -> p a t", a=gn)[:, :, 0:nt]
                eng = "act" if (g // 4) % 2 == 0 else "dve"
                if xt is not None:
                    dst = xt[:, g:g + gn, col0 + i * 128:col0 + i * 128 + nt]
                    if eng == "act":
                        P.op("act", lambda e: e.activation(out=dst, in_=src_ps, func=AF.Copy), reads=bk, writes=[xt_key], merge=True)
                    else:
                        P.op("dve", lambda e: e.tensor_copy(out=dst, in_=src_ps), reads=bk, writes=[xt_key], merge=True)
                if f32_dst is not None:
                    eng2 = "dve" if eng == "act" else "act"
                    dst2 = fb[:, g:g + gn, 0:nt]
                    if eng2 == "act":
                        P.op("act", lambda e: e.activation(out=dst2, in_=src_ps, func=AF.Copy), reads=bk, writes=[fk], merge=(g > 0))
                    else:
                        P.op("dve", lambda e: e.tensor_copy(out=dst2, in_=src_ps), reads=bk, writes=[fk], merge=(g > 0))
            if f32_dst is not None:
                P.dma(f32_dst[i][:, :, 0:nt], fb[:, :, 0:nt], reads=[fk], writes=[tag + "_f32"], sem=f"xf{fk[1]}", merge=True)

    def alloc_mm(self):
        P, c = self.P, self.c
        KH = (c.KT + 1) // 2
        self._wst = Ring("wst", [P.sbuf(f"wst{i}", [128, KH, 128], F32) for i in range(3)])
        self._wbf = Ring("wbf", [P.sbuf(f"wbf{i}", [128, c.KT, 128], BF16) for i in range(2)])
        self._st32 = self.stager("st32_", 2, [128, 512], F32)
        self._st16 = self.stager("st16_", 2, [128, 512], BF16)
        self._lda = self.stager("lda_", 4, [128, 512], F32)
        self._ldb = self.stager("ldb_", 4, [128, 512], F32)
        self._tmp = self.stager("tmp_", 2, [128, 512], F32)

    def mm_phase(self, xt, xt_keys, KT, tok_tiles, Wt, jlist, msz, epi, swap=False):
        P = self.P
        halves = [(0, (KT + 1) // 2), ((KT + 1) // 2, KT)] if KT > 1 else [(0, 1)]
        items = [(idx, h) for idx in range(len(jlist)) for h in range(len(halves))]
        PF = 3
        loaded = {}

        def load(ii):
            if ii >= len(items):
                return
            idx, h = items[ii]
            k0, k1 = halves[h]
            sk, sb = self._wst.next()
            P.dma(sb[:, 0:k1 - k0, :], Wt[jlist[idx]][:, k0:k1, :], writes=[sk], sem=f"wst{sk[1]}")
            loaded[ii] = (sk, sb)

        for ii in range(min(PF, len(items))):
            load(ii)
        ii = 0
        pfn = getattr(epi, "prefetch", None)
        tiles = [(j, msz(j), c0, n) for j in jlist for (c0, n) in tok_tiles]
        PFE = 2
        if pfn is not None:
            for t_ in tiles[:PFE]:
                pfn(*t_)
        tcount = 0
        for idx, j in enumerate(jlist):
            m = msz(j)
            wk, wb = self._wbf.next()
            for h, (k0, k1) in enumerate(halves):
                sk, sb = loaded.pop(ii)
                P.op("pool", lambda e: e.tensor_copy(out=wb[:, k0:k1, :], in_=sb[:, 0:k1 - k0, :]), reads=[sk], writes=[wk], merge=(h > 0))
                load(ii + PF)
                ii += 1
            for (c0, n) in tok_tiles:
                b = self.bank()
                bk = self.bank_keys(b)
                if not swap:
                    pso = self.ps[b][0:m, 0:n]
                else:
                    pso = self.ps[b][0:n, 0:m]
                for k in range(KT):
                    if not swap:
                        fn = lambda e, k=k: e.matmul(pso, lhsT=wb[:, k, 0:m], rhs=xt[:, k, c0:c0 + n], start=(k == 0), stop=(k == KT - 1))
                    else:
                        fn = lambda e, k=k: e.matmul(pso, lhsT=xt[:, k, c0:c0 + n], rhs=wb[:, k, 0:m], start=(k == 0), stop=(k == KT - 1))
                    P.op("pe", fn, reads=[wk] + (list(xt_keys) if k == 0 else []), writes=bk, inc=(k == KT - 1), merge=(k > 0))
                if pfn is not None and tcount + PFE < len(tiles):
                    pfn(*tiles[tcount + PFE])
                tcount += 1
                epi(j, m, c0, n, pso, bk)

    def stager(self, name, n, shape, dt):
        return Ring(name, [self.P.sbuf(f"{name}{i}", shape, dt) for i in range(n)])

    def std_epilogue(self, out_fn, out_dt, out_key, gate_fn=None, res_fn=None, res_scale=1.0, res_key=None, gate_key=None,
                     act=None, out_tiled=False, res_tiled=False):
        P = self
        PP = self.P
        nm = f"ep{getattr(self, '_epn', 0)}"
        self._epn = getattr(self, '_epn', 0) + 1
        cnt = [0]
        pre = {}

        def prefetch(j, m, c0, n):
            gk = gb = rk = rb = None
            if gate_fn is not None:
                gk, gb = self._lda.next()
                PP.dma(gb[0:m, 0:n], gate_fn(j, m, c0, n), reads=[gate_key], writes=[gk], sem=f"lda{gk[1]}")
            if res_fn is not None:
                rk, rb = self._ldb.next()
                rdst = rb[0:m, 0:n].rearrange("p (q t) -> p q t", t=128) if res_tiled else rb[0:m, 0:n]
                PP.dma(rdst, res_fn(j, m, c0, n), reads=[res_key], writes=[rk], sem=f"ldb{rk[1]}")
            pre[(j, c0)] = (gk, gb, rk, rb)

        def epi(j, m, c0, n, pso, bk):
            i = cnt[0]; cnt[0] += 1
            sk, sb = (self._st32 if out_dt == F32 else self._st16).next()
            so = sb[0:m, 0:n]
            if (j, c0) not in pre:
                prefetch(j, m, c0, n)
            gk, gb, rk, rb = pre.pop((j, c0))
            if gate_fn is not None:
                PP.op("act", lambda e: e.activation(out=gb[0:m, 0:n], in_=gb[0:m, 0:n], func=AF.Sigmoid), reads=[gk], writes=[gk])
                if res_fn is None:
                    PP.op("dve", lambda e: e.tensor_tensor(out=so, in0=pso, in1=gb[0:m, 0:n], op=ALU.mult), reads=bk + [gk], writes=[sk])
                else:
                    PP.op("dve", lambda e: e.tensor_tensor(out=gb[0:m, 0:n], in0=pso, in1=gb[0:m, 0:n], op=ALU.mult), reads=bk + [gk], writes=[gk])
                    PP.op("pool", lambda e: e.tensor_tensor(out=so, in0=gb[0:m, 0:n], in1=rb[0:m, 0:n], op=ALU.add), reads=[gk, rk], writes=[sk])
            elif res_fn is not None:
                PP.op("dve", lambda e: e.scalar_tensor_tensor(out=so, in0=rb[0:m, 0:n], scalar=float(res_scale), in1=pso,
                                                              op0=ALU.mult, op1=ALU.add), reads=bk + [rk], writes=[sk])
            elif act == "relu2":
                tk, tb = self._tmp.next()
                PP.op("dve", lambda e: e.tensor_scalar(out=tb[0:m, 0:n], in0=pso, scalar1=0.0, scalar2=None, op0=ALU.max), reads=bk, writes=[tk])
                PP.op("act", lambda e: e.activation(out=so, in_=tb[0:m, 0:n], func=AF.Square), reads=[tk], writes=[sk])
            else:
                if i % 2 == 0:
                    PP.op("act", lambda e: e.activation(out=so, in_=pso, func=AF.Copy), reads=bk, writes=[sk])
                else:
                    PP.op("dve", lambda e: e.tensor_copy(out=so, in_=pso), reads=bk, writes=[sk])
            so_d = so.rearrange("p (q t) -> p q t", t=128) if out_tiled else so
            PP.dma(out_fn(j, m, c0, n), so_d, reads=[sk], writes=[out_key], sem=f"{sk[0]}{sk[1]}", merge=True)
        epi.prefetch = prefetch if (gate_fn is not None or res_fn is not None) else None
        return epi

    def load_xt(self, xt, key, src, src_key, KT, ncol, col0=0):
        P = self.P
        srcv = src.rearrange("(k p) t -> p k t", p=128)
        step = max(1, KT // 4)
        first = True
        for i, k0 in enumerate(range(0, KT, step)):
            k1 = min(KT, k0 + step)
            P.dma(xt[:, k0:k1, col0:col0 + ncol], srcv[:, k0:k1, 0:ncol], reads=[src_key], writes=[key], sem=f"xl{i % 4}", merge=not first)
            first = False

    def alloc_ln(self, with_out_bf=False):
        P, c = self.P, self.c
        KTM = c.KT
        self._lny = Ring("lny", [P.sbuf(f"lny{i}", [128, KTM, 128], F32) for i in range(2)])
        self._lnb = Ring("lnb", [P.sbuf(f"lnb{i}", [128, 2, KTM, 128], BF16) for i in range(1)])
        self._lns = Ring("lns", [P.sbuf(f"lns{i}", [128, 4, 128], F32) for i in range(2)])
        if with_out_bf:
            self._lno = Ring("lno", [P.sbuf(f"lno{i}", [128, KTM, 128], BF16) for i in range(2)])

    def ln_phase(self, src, src_key, KTn, ntok, gb, ones_idx, eps, out_f32=None, out_f32_key=None, xt=None, xt_key=None,
                 out_bf=None, out_bf_key=None, silu=False, final_out=None, name="ln"):
        P, c = self.P, self.c
        KTM = c.KT
        ones = self.ones_b[:, ones_idx, :]
        for ti in range(ntok // 128):
            c0 = ti * 128
            yk, yb = self._lny.next()
            Y = yb[:, 0:KTn, :]
            P.dma(Y, src[ti], reads=[src_key], writes=[yk], sem=f"lny{yk[1]}")
            bk_, bb = self._lnb.next()
            P.op("pool", lambda e: e.tensor_copy(out=bb[:, 0, 0:KTn, :], in_=Y), reads=[yk], writes=[(bk_, 0)])
            P.op("act", lambda e: e.activation(out=bb[:, 1, 0:KTn, :], in_=Y, func=AF.Square), reads=[yk], writes=[(bk_, 1)])
            pk = []
            for s in range(2):
                hk, hp = self.half()
                for k in range(KTn):
                    P.op("pe", lambda e, k=k: e.matmul(hp[:, 0:128], lhsT=ones, rhs=bb[:, s, k, :], start=(k == 0), stop=(k == KTn - 1)),
                         reads=[(bk_, s), f"ones{ones_idx}"], writes=[hk], inc=(k == KTn - 1), merge=(k > 0))
                pk.append((hk, hp[:, 0:128]))
            sk, st = self._lns.next()
            mean, msq = st[:, 0, :], st[:, 1, :]
            kR, pR = self.half()
            rstd, m2 = pR[:, 0:128], pR[:, 128:256]
            P.op("act", lambda e: e.activation(out=mean, in_=pk[0][1], func=AF.Copy), reads=[pk[0][0]], writes=[(sk, 0)])
            P.op("dve", lambda e: e.tensor_tensor(out=msq, in0=mean, in1=mean, op=ALU.mult), reads=[(sk, 0)], writes=[(sk, 1)])
            P.op("dve", lambda e: e.tensor_tensor(out=msq, in0=pk[1][1], in1=msq, op=ALU.subtract), reads=[pk[1][0], (sk, 1)], writes=[(sk, 1)])
            P.op("dve", lambda e: e.tensor_scalar(out=msq, in0=msq, scalar1=float(eps), scalar2=None, op0=ALU.add), reads=[(sk, 1)], writes=[(sk, 1)])
            P.op("act", lambda e: e.activation(out=msq, in_=msq, func=AF.Sqrt), reads=[(sk, 1)], writes=[(sk, 1)])
            P.op("dve", lambda e: e.reciprocal(out=rstd, in_=msq), reads=[(sk, 1)], writes=[kR])
            P.op("dve", lambda e: e.tensor_tensor(out=m2, in0=mean, in1=rstd, op=ALU.mult), reads=[(sk, 0), kR], writes=[kR])
            rb = rstd.unsqueeze(1).broadcast_to([128, KTn, 128])
            mb = m2.unsqueeze(1).broadcast_to([128, KTn, 128])
            P.op("dve", lambda e: e.tensor_tensor(out=Y, in0=Y, in1=rb, op=ALU.mult), reads=[yk, kR], writes=[yk])
            P.op("dve", lambda e: e.tensor_tensor(out=Y, in0=Y, in1=mb, op=ALU.subtract), reads=[yk, kR], writes=[yk])
            fn_ = AF.Silu if silu else AF.Identity
            need_f32 = (out_f32 is not None) or (final_out is not None)
            if out_bf is not None:
                ok, ob = self._lno.next()
            for k in range(KTn):
                if need_f32:
                    dstk = Y[:, k, :]; wk_ = [yk]
                elif xt is not None:
                    dstk = xt[:, k, c0:c0 + 128]; wk_ = [xt_key]
                else:
                    dstk = ob[:, k, :]; wk_ = [ok]
                P.op("act", lambda e, k=k: e.activation(out=dstk, in_=Y[:, k, :], func=fn_, scale=gb[:, 0, k:k + 1], bias=gb[:, 1, k:k + 1]),
                     reads=[yk, "gb_" + name], writes=wk_, merge=True)
            if out_f32 is not None:
                P.dma(out_f32[ti], Y, reads=[yk], writes=[out_f32_key], sem=f"lnyo{yk[1]}", merge=True)
            if need_f32 and xt is not None:
                P.op("pool", lambda e: e.tensor_copy(out=xt[:, 0:KTn, c0:c0 + 128], in_=Y), reads=[yk], writes=[xt_key], merge=True)
            if need_f32 and out_bf is not None:
                P.op("pool", lambda e: e.tensor_copy(out=ob[:, 0:KTn, :], in_=Y), reads=[yk], writes=[ok])
            if out_bf is not None:
                P.dma(out_bf.rearrange("(k p) t -> p k t", p=128)[:, :, c0:c0 + 128], ob[:, 0:KTn, :], reads=[ok], writes=[out_bf_key],
                      sem=f"lno{ok[1]}", merge=True)
            if final_out is not None:
                xk, xb = self._xin.next()
                for g in range(0, KTn, 4):
                    b = self.bank()
                    bk = self.bank_keys(b)
                    for q in range(4):
                        P.op("pe", lambda e, q=q: e.transpose(self.ps[b][:, q * 128:(q + 1) * 128], yb[:, g + q, :], self.ident_f),
                             reads=[yk, "cf"], writes=bk, inc=(q == 3), merge=(q > 0))
                    if (g // 4) % 2 == 0:
                        P.op("act", lambda e: e.activation(out=xb[:, g * 128:(g + 4) * 128], in_=self.ps[b][:, :], func=AF.Copy), reads=bk, writes=[xk], merge=(g > 0))
                    else:
                        P.op("dve", lambda e: e.tensor_copy(out=xb[:, g * 128:(g + 4) * 128], in_=self.ps[b][:, :]), reads=bk, writes=[xk], merge=(g > 0))
                P.dma(final_out[c0:c0 + 128, :], xb[:, :], reads=[xk], writes=["final"], sem=f"xin{xk[1]}", merge=True)

    def conv_phase(self, Z, convw, convb, uout):
        P, c = self.P, self.c
        T, TH = c.T, c.TH
        NCT = c.DC // 128
        za = Ring("cva", [P.sbuf(f"cva{i}", [128, TH], F32) for i in range(2)])
        zg = Ring("cvg", [P.sbuf(f"cvg{i}", [128, TH], F32) for i in range(2)])
        ub = Ring("cvu", [P.sbuf(f"cvu{i}", [128, TH], BF16) for i in range(2)])
        dg = Ring("cvd", [P.sbuf(f"cvd{i}", [128, c.CW, 128], BF16) for i in range(2)])
        for ct in range(NCT):
            ak, ab = za.next(); gk, gbf = zg.next(); uk, ubf = ub.next(); dk, dgt = dg.next()
            r0 = c.C_CONV + ct * 128
            P.dma(ab[:, :], Z[r0:r0 + 128, :], reads=["Z"], writes=[ak], sem=f"cva{ak[1]}")
            P.dma(gbf[:, :], Z[r0 + c.DC:r0 + c.DC + 128, :], reads=["Z"], writes=[gk], sem=f"cvg{gk[1]}")
            P.op("act", lambda e: e.activation(out=gbf[:, :], in_=gbf[:, :], func=AF.Sigmoid), reads=[gk], writes=[gk])
            P.op("dve", lambda e: e.tensor_tensor(out=ubf[:, :], in0=ab[:, :], in1=gbf[:, :], op=ALU.mult), reads=[ak, gk], writes=[uk])
            for j in range(c.CW):
                P.op("pool", lambda e, j=j: e.tensor_scalar(out=dgt[:, j, :], in0=self.ident_f, scalar1=convw[:, ct, j:j + 1], scalar2=None, op0=ALU.mult),
                     reads=["cf", "convw"], writes=[dk], merge=(j > 0))
            for tt in range(T // c.TT):
                b = self.bank(); bk = self.bank_keys(b)
                for j in range(c.CW):
                    cs = 2 + j + tt * c.TT
                    P.op("pe", lambda e, j=j, cs=cs: e.matmul(self.ps[b][:, :], lhsT=dgt[:, j, :], rhs=ubf[:, cs:cs + c.TT],
                                                             start=(j == 0), stop=(j == c.CW - 1)),
                         reads=[dk, uk], writes=bk, inc=(j == c.CW - 1), merge=(j > 0))
                sk, sb = self._st32.next()
                P.op("act", lambda e: e.activation(out=sb[:, :], in_=self.ps[b][:, :], func=AF.Identity, bias=convb[:, ct:ct + 1]),
                     reads=bk + ["convb"], writes=[sk])
                P.dma(uout[tt * (c.TT // 128):(tt + 1) * (c.TT // 128), :, ct, :].rearrange("q p t -> p q t"),
                      sb[:, :].rearrange("p (q t) -> p q t", t=128), reads=[sk], writes=["uconv"], sem=f"{sk[0]}{sk[1]}", merge=True)

    def attn_phase(self, qT, kT, vtm, oT):
        P, c = self.P, self.c
        XD = c.XD; KH = XD // 128; MT = c.MEM // 128
        scale = float(XD ** -0.5)
        qh = Ring("qh", [P.sbuf(f"qh{i}", [128, KH, c.T], BF16) for i in range(2)])
        pt = Ring("ptb", [P.sbuf(f"ptb{i}", [128, MT, c.TT], BF16) for i in range(2)])
        pf = Ring("pf", [P.sbuf(f"pf{i}", [128, c.MEM], F32) for i in range(2)])
        pb = Ring("pb", [P.sbuf(f"pb{i}", [128, c.MEM], BF16) for i in range(2)])
        sm = Ring("sm", [P.sbuf(f"sm{i}", [128, 4], F32) for i in range(4)])
        qv = qT.rearrange("(k p) t -> p k t", p=128)
        for h in range(c.XH):
            qk, qb = qh.next()
            P.dma(qb[:, :, :], qv[:, h * KH:(h + 1) * KH, :], reads=["qT"], writes=[qk], sem=f"qh{qk[1]}")
            for g in range(c.T // c.TT):
                ptk, ptb = pt.next()
                for i in range(c.TT // 128):
                    t0 = g * c.TT + i * 128
                    hk, hp = self.half()
                    for k in range(KH):
                        P.op("pe", lambda e, k=k: e.matmul(hp[:, 0:c.MEM], lhsT=qb[:, k, t0:t0 + 128], rhs=kT[:, h * KH + k, :],
                                                           start=(k == 0), stop=(k == KH - 1)),
                             reads=[qk, "kT"], writes=[hk], inc=(k == KH - 1), merge=(k > 0))
                    sk, st = sm.next()
                    P.op("dve", lambda e: e.tensor_reduce(out=st[:, 0:1], in_=hp[:, 0:c.MEM], axis=AX.X, op=ALU.max), reads=[hk], writes=[sk])
                    P.op("dve", lambda e: e.tensor_scalar(out=st[:, 1:2], in0=st[:, 0:1], scalar1=-scale, scalar2=None, op0=ALU.mult), reads=[sk], writes=[sk])
                    fk, fb = pf.next()
                    P.op("act", lambda e: e.activation(out=fb[:, :], in_=hp[:, 0:c.MEM], func=AF.Exp, bias=st[:, 1:2], scale=scale, accum_out=st[:, 2:3]),
                         reads=[hk, sk], writes=[fk, sk])
                    P.op("dve", lambda e: e.reciprocal(out=st[:, 3:4], in_=st[:, 2:3]), reads=[sk], writes=[sk])
                    bk_, bb = pb.next()
                    P.op("dve", lambda e: e.tensor_scalar(out=bb[:, :], in0=fb[:, :], scalar1=st[:, 3:4], scalar2=None, op0=ALU.mult), reads=[fk, sk], writes=[bk_])
                    tk, tp = self.half()
                    tpb = tp.bitcast(BF16)
                    for mt in range(MT):
                        P.op("pe", lambda e, mt=mt: e.transpose(tpb[:, mt * 128:(mt + 1) * 128], bb[:, mt * 128:(mt + 1) * 128], self.ident_b),
                             reads=[bk_, "cb"], writes=[tk], inc=(mt == MT - 1), merge=(mt > 0))
                    P.op("act", lambda e: e.activation(out=ptb[:, :, i * 128:(i + 1) * 128],
                                                       in_=tpb[:, 0:MT * 128].rearrange("p (a t) -> p a t", a=MT), func=AF.Copy),
                         reads=[tk], writes=[ptk], merge=(i > 0))
                for dv in range(KH):
                    b = self.bank(); bk = self.bank_keys(b)
                    col = h * XD + dv * 128
                    for mt in range(MT):
                        P.op("pe", lambda e, mt=mt: e.matmul(self.ps[b][:, :], lhsT=vtm[:, mt, col:col + 128], rhs=ptb[:, mt, :],
                                                             start=(mt == 0), stop=(mt == MT - 1)),
                             reads=["vtm", ptk], writes=bk, inc=(mt == MT - 1), merge=(mt > 0))
                    sk2, sb2 = self._st16.next()
                    if dv % 2 == 0:
                        P.op("act", lambda e: e.activation(out=sb2[:, :], in_=self.ps[b][:, :], func=AF.Copy), reads=bk, writes=[sk2])
                    else:
                        P.op("dve", lambda e: e.tensor_copy(out=sb2[:, :], in_=self.ps[b][:, :]), reads=bk, writes=[sk2])
                    P.dma(oT[col:col + 128, g * c.TT:(g + 1) * c.TT], sb2[:, :], reads=[sk2], writes=["oT"], sem=f"{sk2[0]}{sk2[1]}", merge=True)

    def rwkv_phase(self, Z, ZP, orT):
        P, c = self.P, self.c
        SEG = c.SEG; NCH = SEG // 64; NHP = c.NHP; DR = c.DR
        nseg = c.T // SEG
        segs = [("p", s) for s in range(nseg)] + [("m", s) for s in range(nseg)]
        NT = SEG // 512 if SEG >= 512 else 1
        TW = min(512, SEG)
        NV = 10 * NHP + 4
        rvec_in = self.inp("rvec", [128, NV])
        rv = P.sbuf("rv", [128, NV], F32)
        P.dma(rv[:, :], rvec_in[:, :], writes=["rv"], sem="crv")
        MU_R, MU_K, MU_V, W0, A0, KK, KA, RK, GG, GB = range(10)
        pv = lambda i, hp: rv[:, i * NHP + hp:i * NHP + hp + 1]
        omka = P.sbuf("omka", [128, NHP], F32)
        P.op("dve", lambda e: e.tensor_scalar(out=omka[:, :], in0=rv[:, KA * NHP:(KA + 1) * NHP], scalar1=-1.0, scalar2=1.0, op0=ALU.mult, op1=ALU.add),
             reads=["rv"], writes=["omka"])
        W1 = SEG + 1
        zK = P.sbuf("zK", [128, W1], F32); zV = P.sbuf("zV", [128, W1], F32); zR = P.sbuf("zR", [128, W1], F32)
        AI = P.sbuf("AI", [128, SEG], F32)
        t1 = P.sbuf("t1", [128, SEG], F32); t2 = P.sbuf("t2", [128, SEG], F32)
        t3 = P.sbuf("t3", [128, SEG], F32); t4 = P.sbuf("t4", [128, SEG], F32)
        wup_in = self.inp("w_up_t", [128, DR]); aup_in = self.inp("a_up_t", [128, DR]); gup_in = self.inp("g_up_t", [128, 2, DR])
        wup_b = P.sbuf("wup_b", [128, DR], BF16); aup_b = P.sbuf("aup_b", [128, DR], BF16); gup_b = P.sbuf("gup_b", [128, 2, DR], BF16)
        for (src, dst, nm) in [(wup_in[:, :], wup_b[:, :], "wup"), (aup_in[:, :], aup_b[:, :], "aup"),
                               (gup_in[:, 0, :], gup_b[:, 0, :], "gup0"), (gup_in[:, 1, :], gup_b[:, 1, :], "gup1")]:
            for o in range(0, DR, SEG):
                w = min(SEG, DR - o)
                P.dma(t1[:, 0:w], src[:, o:o + w], writes=["t1"], sem="c0")
                P.op("dve", lambda e: e.tensor_copy(out=dst[:, o:o + w], in_=t1[:, 0:w]), reads=["t1"], writes=[nm], merge=True)
        tdw = P.sbuf("tdw", [128, 2 * c.T], BF16); sda = P.sbuf("sda", [128, 2 * c.T], BF16); sdg = P.sbuf("sdg", [128, 2, c.T], BF16)

        def shift(zt, zkey, mu_ap, out_ap, okey, np_=128):
            P.op("pool", lambda e: e.tensor_tensor(out=t1[0:np_, 0:SEG], in0=zt[0:np_, 0:SEG], in1=zt[0:np_, 1:SEG + 1], op=ALU.subtract),
                 reads=[zkey], writes=["t1"])
            P.op("dve", lambda e: e.scalar_tensor_tensor(out=out_ap, in0=t1[0:np_, 0:SEG], scalar=mu_ap, in1=zt[0:np_, 1:SEG + 1],
                                                         op0=ALU.mult, op1=ALU.add), reads=["t1", zkey], writes=[okey])

        for si, (kind, s) in enumerate(segs):
            src = ZP if kind == "p" else Z
            skey = "ZP" if kind == "p" else "Z"
            cb0 = 32 + s * SEG
            off = si * SEG
            mdw = rv[0:96, 10 * NHP:10 * NHP + 1]; mda = rv[0:96, 10 * NHP + 1:10 * NHP + 2]
            P.dma(zK[0:96, :], src[c.C_DW:c.C_DW + 96, cb0 - 1:cb0 + SEG], reads=[skey], writes=["zK"], sem="zK")
            shift(zK, "zK", mdw, zK[0:96, 1:SEG + 1], "zK", 96)
            P.op("act", lambda e: e.activation(out=tdw[0:96, off:off + SEG], in_=zK[0:96, 1:SEG + 1], func=AF.Tanh), reads=["zK"], writes=["tdw"], merge=True)
            P.dma(zV[0:96, :], src[c.C_DA:c.C_DA + 96, cb0 - 1:cb0 + SEG], reads=[skey], writes=["zV"], sem="zV")
            shift(zV, "zV", mda, zV[0:96, 1:SEG + 1], "zV", 96)
            P.op("act", lambda e: e.activation(out=sda[0:96, off:off + SEG], in_=zV[0:96, 1:SEG + 1], func=AF.Copy), reads=["zV"], writes=["sda"], merge=True)
            if kind == "m":
                for k in range(2):
                    mdg = rv[:, 10 * NHP + 2 + k:10 * NHP + 3 + k]
                    P.dma(zR[:, :], src[c.C_DG + k * 128:c.C_DG + (k + 1) * 128, cb0 - 1:cb0 + SEG], reads=[skey], writes=["zR"], sem="zR")
                    shift(zR, "zR", mdg, zR[:, 1:SEG + 1], "zR")
                    P.op("act", lambda e: e.activation(out=sdg[:, k, s * SEG:(s + 1) * SEG], in_=zR[:, 1:SEG + 1], func=AF.Sigmoid),
                         reads=["zR"], writes=["sdg"], merge=True)
        BDAR = [P.sbuf(f"BDAR{i}", [128, NCH, 2, 128], BF16) for i in range(2)]
        BDB = [P.sbuf(f"BDB{i}", [128, NCH, 128], BF16) for i in range(2)]
        BDK = [P.sbuf(f"BDK{i}", [128, NCH, 128], BF16) for i in range(2)]
        BDV = [P.sbuf(f"BDV{i}", [128, NCH, 128], BF16) for i in range(2)]
        E1 = [P.sbuf(f"E1_{i}", [128, SEG], F32) for i in range(3)]
        GB_ = [P.sbuf(f"G_{i}", [128, SEG], F32) for i in range(4)]
        BON = [P.sbuf(f"BON{i}", [128, SEG], F32) for i in range(4)]
        t5 = P.sbuf("t5", [128, SEG], F32)
        OTM = [P.sbuf(f"OTM{i}", [128, NCH, 64], F32) for i in range(2)]
        OBD = P.sbuf("OBD", [128, NCH, 128], BF16)
        gst = P.sbuf("gst", [128, 4, NCH], F32)
        ost = Ring("ost", [P.sbuf(f"ost{i}", [128, 256], BF16) for i in range(2)])
        G = 4
        fctx = []
        for i in range(2):
            d = dict(
                X0T=P.sbuf(f"fX0T{i}", [128, G, 128], BF16), X0=P.sbuf(f"fX0{i}", [128, G, 128], BF16),
                YX=[P.sbuf(f"fYX{i}_{q}", [128, G, 256], BF16) for q in range(2)],
                XT=[P.sbuf(f"fXT{i}_{q}", [128, G, 128], BF16) for q in range(2)],
                MKT=P.sbuf(f"fMKT{i}", [128, G, 128], BF16), ARK=P.sbuf(f"fARK{i}", [128, G, 128], BF16),
                ARB=P.sbuf(f"fARB{i}", [128, G, 128], BF16),
                TM=P.sbuf(f"fTM{i}", [128, G, 448], BF16), TIT=P.sbuf(f"fTIT{i}", [128, G, 128], BF16),
                Q=P.sbuf(f"fQ{i}", [128, G, 64], BF16), BDP=P.sbuf(f"fBDP{i}", [128, G, 128], BF16), i=i)
            P.op("pool", lambda e: e.memset(d["BDP"][:, :, :], 0.0), writes=[("f", i, "BDP")])
            fctx.append(d)
        bctx = [dict(GT=P.sbuf(f"bGT{i}", [128, G, 128], F32), MUN=P.sbuf(f"bMUN{i}", [128, G, 128], F32),
                     H=P.sbuf(f"bH{i}", [128, G, 64], F32), NUN=P.sbuf(f"bNUN{i}", [128, G, 64], F32), i=i) for i in range(4)]
        STS = [[P.sbuf(f"ST{a}_{b}", [128, 64], F32) for b in range(2)] for a in range(2)]
        hm4 = self.hmask.unsqueeze(1).unsqueeze(3).broadcast_to([128, NCH, 2, 64])
        hmn4 = self.hmaskn.unsqueeze(1).unsqueeze(3).broadcast_to([128, NCH, 2, 64])

        def bd(eng, x_ap, out4, mask4, rkeys, wkey):
            x4 = x_ap.rearrange("p (c t) -> p c t", t=64).unsqueeze(2).broadcast_to([128, NCH, 2, 64])
            P.op(eng, lambda e: e.tensor_tensor(out=out4, in0=x4, in1=mask4, op=ALU.mult), reads=list(rkeys) + ["cf"], writes=[wkey])

        def v4(t3d):
            return t3d.rearrange("p c (h t) -> p c h t", h=2)

        def lora_tile(wb, wkey, rhs, rkey, hp, off, tt, nk):
            b = self.bank(); bk = self.bank_keys(b)
            for k in range(nk):
                if nk == 1:
                    l = wb[0:96, hp * 128:(hp + 1) * 128]; r = rhs[0:96, off + tt * TW:off + (tt + 1) * TW]
                else:
                    l = wb[:, k, hp * 128:(hp + 1) * 128]; r = rhs[:, k, off + tt * TW:off + (tt + 1) * TW]
                P.op("pe", lambda e, k=k: e.matmul(self.ps[b][:, 0:TW], lhsT=l, rhs=r, start=(k == 0), stop=(k == nk - 1)),
                     reads=[wkey, rkey], writes=bk, inc=(k == nk - 1), merge=(k > 0))
            return bk, self.ps[b][:, 0:TW]

        def headsum_tile(x, xkey, tt):
            b = self.bank(); bk = self.bank_keys(b)
            P.op("pe", lambda e: e.matmul(self.ps[b][:, 0:TW], lhsT=self.bones_f, rhs=x[:, tt * TW:(tt + 1) * TW], start=True, stop=True),
                 reads=["cf", xkey], writes=bk)
            return bk, self.ps[b][:, 0:TW]

        stq = [0, 0]
        bci = [0]
        pend_chain = []
        pend_post = []
        m3 = lambda m: m.unsqueeze(1).broadcast_to([128, G, 128])
        g3 = lambda ap, w: ap.rearrange("p (g t) -> p g t", g=G)[:, :, 0:w] if w else ap.rearrange("p (g t) -> p g t", g=G)

        def fk(grp, f):
            return ("f", grp["x"]["i"], f)

        def bkk(grp, f):
            return ("b", grp["y"]["i"], f)

        def rd(grp):
            d = grp["itd"]; q = d["q2"]
            return [("BDA", q), ("BDB", q), ("BDK", q), ("BDV", q)] + ([("BDR", q)] if d["main"] else [])

        def gmm(grp, w, lhs_fn, rhs_fn, reads, nacc=1):
            b = self.bank(); bk = ("psb", b)
            for g in range(G):
                for a in range(nacc):
                    P.op("pe", lambda e, g=g, a=a: e.matmul(self.ps[b][:, g * w:(g + 1) * w], lhsT=lhs_fn(g, a), rhs=rhs_fn(g, a),
                                                          start=(a == 0), stop=(a == nacc - 1)),
                         reads=reads, writes=[bk], inc=(g == G - 1 and a == nacc - 1), merge=True)
            return bk, self.ps[b][:, 0:G * w].rearrange("p (g t) -> p g t", g=G)

        def stage1(grp):
            d = grp["itd"]; x = grp["x"]; c0_ = grp["ch0"]; main_ = d["main"]
            bdar_, bdb_, bdk_, bdv_ = d["bdar"], d["bdb"], d["bdk"], d["bdv"]
            R_ = rd(grp)
            kA, pA = gmm(grp, 128, lambda g, a: bdar_[:, c0_ + g, 0, :], lambda g, a: bdb_[:, c0_ + g, :], R_)
            P.op("dve", lambda e: e.tensor_tensor(out=x["X0T"][:, :, :], in0=pA, in1=m3(self.mask_sl), op=ALU.mult), reads=[kA, "cf"], writes=[fk(grp, "X0T")])
            kB, pB = gmm(grp, 128, lambda g, a: bdb_[:, c0_ + g, :], lambda g, a: bdar_[:, c0_ + g, 0, :], R_)
            P.op("dve", lambda e: e.tensor_tensor(out=x["X0"][:, :, :], in0=pB, in1=m3(self.mask_su), op=ALU.mult), reads=[kB, "cf"], writes=[fk(grp, "X0")])
            P.op("pool", lambda e: e.tensor_tensor(out=x["YX"][1][:, :, 0:128], in0=x["X0"][:, :, :], in1=m3(self.ident_b), op=ALU.add),
                 reads=[fk(grp, "X0"), "cb"], writes=[fk(grp, "YXy1")])
            kC, pC = gmm(grp, 128, lambda g, a: bdk_[:, c0_ + g, :], lambda g, a: bdar_[:, c0_ + g, 0, :], R_)
            P.op("dve", lambda e: e.tensor_tensor(out=x["MKT"][:, :, :], in0=pC, in1=m3(self.mask_su), op=ALU.mult), reads=[kC, "cf"], writes=[fk(grp, "MKT")])
            if main_:
                mu_ = self.mask_su_u[:, 128:256]
                kB2, pB2 = gmm(grp, 128, lambda g, a: bdb_[:, c0_ + g, :], lambda g, a: bdar_[:, c0_ + g, 1, :], R_)
                P.op("dve", lambda e: e.tensor_tensor(out=x["ARB"][:, :, :], in0=pB2, in1=m3(mu_), op=ALU.mult), reads=[kB2, "cf"], writes=[fk(grp, "ARB")])
                kC2, pC2 = gmm(grp, 128, lambda g, a: bdk_[:, c0_ + g, :], lambda g, a: bdar_[:, c0_ + g, 1, :], R_)
                P.op("dve", lambda e: e.tensor_tensor(out=x["ARK"][:, :, :], in0=pC2, in1=m3(mu_), op=ALU.mult), reads=[kC2, "cf"], writes=[fk(grp, "ARK")])
            b = self.bank(); kD = ("psb", b)
            for g in range(G):
                P.op("pe", lambda e, g=g: e.matmul(self.ps[b][:, g * 128:g * 128 + 64], lhsT=bdar_[:, c0_ + g, 0, :], rhs=self.isel_b, start=True, stop=True),
                     reads=R_ + ["cb"], writes=[kD], inc=False, merge=True)
                P.op("pe", lambda e, g=g: e.matmul(self.ps[b][:, g * 128 + 64:g * 128 + 128], lhsT=bdv_[:, c0_ + g, :], rhs=self.isel_b, start=True, stop=True),
                     reads=R_ + ["cb"], writes=[kD], inc=(g == G - 1), merge=True)
            pDv = self.ps[b][:, :].rearrange("p (g a t) -> p g a t", g=G, a=2)
            tmv = x["TM"][:, :, 0:256].rearrange("p g (a t) -> p g a t", a=2)[:, :, :, 0:64]
            P.op("act", lambda e: e.activation(out=tmv, in_=pDv, func=AF.Copy), reads=[kD], writes=[fk(grp, "TMa")])
            kE, pE = gmm(grp, 128, lambda g, a: bdb_[:, c0_ + g, :], lambda g, a: self.ident_b, R_ + ["cb"])
            P.op("act", lambda e: e.activation(out=x["TM"][:, :, 192:320], in_=pE, func=AF.Copy), reads=[kE], writes=[fk(grp, "TMb")])
            kE2, pE2 = gmm(grp, 128, lambda g, a: bdk_[:, c0_ + g, :], lambda g, a: self.ident_b, R_ + ["cb"])
            P.op("act", lambda e: e.activation(out=x["TM"][:, :, 320:448], in_=pE2, func=AF.Copy), reads=[kE2], writes=[fk(grp, "TMk")])

        def stage2(grp):
            x = grp["x"]
            kF, pF = gmm(grp, 128, lambda g, a: x["X0T"][:, g, :], lambda g, a: x["X0"][:, g, :], [fk(grp, "X0T"), fk(grp, "X0")])
            P.op("act", lambda e: e.activation(out=x["YX"][1][:, :, 128:256], in_=pF, func=AF.Copy), reads=[kF], writes=[fk(grp, "YXx1")])
            kG, pG = gmm(grp, 128, lambda g, a: x["X0"][:, g, :], lambda g, a: x["X0T"][:, g, :], [fk(grp, "X0T"), fk(grp, "X0")])
            P.op("act", lambda e: e.activation(out=x["XT"][1][:, :, :], in_=pG, func=AF.Copy), reads=[kG], writes=[fk(grp, "XTq1")])
            kK, pK = gmm(grp, 64, lambda g, a: x["MKT"][:, g, :], lambda g, a: x["TM"][:, g, 128:192], [fk(grp, "MKT"), fk(grp, "TMa")])
            P.op("act", lambda e: e.activation(out=x["TM"][:, :, 64:128], in_=pK, func=AF.Copy), reads=[kK], writes=[fk(grp, "MV")])

        def dstep(grp, q):
            x = grp["x"]; n = 1 - q
            kH, pH = gmm(grp, 128, lambda g, a: x["XT"][q][:, g, :], lambda g, a: x["YX"][q][:, g, 0:128], [fk(grp, f"XTq{q}"), fk(grp, f"YXy{q}")])
            P.op("dve", lambda e: e.tensor_tensor(out=x["YX"][n][:, :, 0:128], in0=pH, in1=x["YX"][q][:, :, 0:128], op=ALU.add),
                 reads=[kH, fk(grp, f"YXy{q}")], writes=[fk(grp, f"YXy{n}")])
            kH2, pH2 = gmm(grp, 128, lambda g, a: x["XT"][q][:, g, :], lambda g, a: x["YX"][q][:, g, 128:256], [fk(grp, f"XTq{q}"), fk(grp, f"YXx{q}")])
            P.op("act", lambda e: e.activation(out=x["YX"][n][:, :, 128:256], in_=pH2, func=AF.Copy), reads=[kH2], writes=[fk(grp, f"YXx{n}")])
            kI, pI = gmm(grp, 128, lambda g, a: x["YX"][q][:, g, 128:256], lambda g, a: x["XT"][q][:, g, :], [fk(grp, f"XTq{q}"), fk(grp, f"YXx{q}")])
            P.op("act", lambda e: e.activation(out=x["XT"][n][:, :, :], in_=pI, func=AF.Copy), reads=[kI], writes=[fk(grp, f"XTq{n}")])

        def stage7(grp):
            x = grp["x"]
            kJ, pJ = gmm(grp, 128, lambda g, a: x["XT"][1][:, g, :], lambda g, a: x["YX"][1][:, g, 0:128], [fk(grp, "XTq1"), fk(grp, "YXy1")])
            P.op("dve", lambda e: e.tensor_tensor(out=x["TIT"][:, :, :], in0=pJ, in1=x["YX"][1][:, :, 0:128], op=ALU.add),
                 reads=[kJ, fk(grp, "YXy1")], writes=[fk(grp, "TIT")])

        def stage8(grp):
            x = grp["x"]
            kL, pL = gmm(grp, 128, lambda g, a: x["TIT"][:, g, :], lambda g, a: x["TM"][:, g, 0:128], [fk(grp, "TIT"), fk(grp, "TMa"), fk(grp, "MV")])
            P.op("act", lambda e: e.activation(out=x["Q"][:, :, :], in_=pL[:, :, 64:128], func=AF.Copy), reads=[kL], writes=[fk(grp, "Q")])
            P.op("act", lambda e: e.activation(out=x["BDP"][0:64, :, 0:64], in_=pL[0:64, :, 0:64], func=AF.Copy), reads=[kL], writes=[fk(grp, "BDP")])
            P.op("act", lambda e: e.activation(out=x["BDP"][64:128, :, 64:128], in_=pL[64:128, :, 0:64], func=AF.Copy), reads=[kL], writes=[fk(grp, "BDP")], merge=True)

        def stage9(grp):
            d = grp["itd"]; x = grp["x"]; y = grp["y"]; c0_ = grp["ch0"]; main_ = d["main"]
            e1_ = d["e1"]; q = d["q2"]
            wcb = e1_[:, c0_ * 64:(c0_ + G) * 64].rearrange("p (g t) -> p g t", t=64)[:, :, 63:64].broadcast_to([128, G, 64])
            if main_:
                kM, pM = gmm(grp, 128, lambda g, a: x["BDP"][:, g, :], lambda g, a: x["ARB"][:, g, :], [fk(grp, "BDP"), fk(grp, "ARB")])
                P.op("dve", lambda e: e.tensor_tensor(out=y["GT"][:, :, :], in0=pM, in1=d["bdar"][:, c0_:c0_ + G, 1, :], op=ALU.add),
                     reads=[kM, ("BDR", q)], writes=[bkk(grp, "GT")])
            kN, pN = gmm(grp, 128, lambda g, a: x["BDP"][:, g, :], lambda g, a: x["TM"][:, g, 192:320], [fk(grp, "BDP"), fk(grp, "TMb")])
            P.op("dve", lambda e: e.tensor_tensor(out=y["MUN"][:, :, :], in0=pN, in1=m3(self.ident_f), op=ALU.add), reads=[kN, "cf"], writes=[bkk(grp, "MUN")])
            if main_:
                kO, pO = gmm(grp, 64, lambda g, a: (x["ARB"][:, g, :] if a == 0 else x["ARK"][:, g, :]),
                             lambda g, a: (x["Q"][:, g, :] if a == 0 else x["TM"][:, g, 128:192]),
                             [fk(grp, "ARB"), fk(grp, "ARK"), fk(grp, "Q"), fk(grp, "TMa")], nacc=2)
                P.op("act", lambda e: e.activation(out=y["H"][:, :, :], in_=pO, func=AF.Copy), reads=[kO], writes=[bkk(grp, "H")])
            kP, pP = gmm(grp, 64, lambda g, a: (x["TM"][:, g, 192:320] if a == 0 else x["TM"][:, g, 320:448]),
                         lambda g, a: (x["Q"][:, g, :] if a == 0 else x["TM"][:, g, 128:192]),
                         [fk(grp, "TMb"), fk(grp, "TMk"), fk(grp, "Q"), fk(grp, "TMa")], nacc=2)
            P.op("dve", lambda e: e.tensor_tensor(out=y["NUN"][:, :, :], in0=pP, in1=wcb, op=ALU.mult), reads=[kP, ("E1", d["e3"])], writes=[bkk(grp, "NUN")])

        def chain_unit(grp, g):
            d = grp["itd"]; y = grp["y"]; ch = grp["ch0"] + g; q = d["q2"]; hpp = d["hp"] % 2
            wc = d["e1"][:, ch * 64 + 63:ch * 64 + 64]
            sq = stq[hpp]
            so, sn = STS[hpp][sq], STS[hpp][1 - sq]
            if d["main"]:
                b = self.bank(); kO2 = ("psb", b); pO2 = self.ps[b][:, 0:64]
                P.op("pe", lambda e: e.matmul(pO2, lhsT=y["GT"][:, g, :], rhs=so[:, :], start=True, stop=True),
                     reads=[bkk(grp, "GT"), ("ST", hpp, sq)], writes=[kO2])
                P.op("dve", lambda e: e.tensor_tensor(out=d["otm"][:, ch, :], in0=pO2, in1=y["H"][:, g, :], op=ALU.add),
                     reads=[kO2, bkk(grp, "H")], writes=[("OTM", q)], merge=True)
            b = self.bank(); kS = ("psb", b); pS = self.ps[b][:, 0:64]
            P.op("pe", lambda e: e.matmul(pS, lhsT=y["MUN"][:, g, :], rhs=so[:, :], start=True, stop=True),
                 reads=[bkk(grp, "MUN"), ("ST", hpp, sq)], writes=[kS])
            P.op("dve", lambda e: e.scalar_tensor_tensor(out=sn[:, :], in0=pS, scalar=wc, in1=y["NUN"][:, g, :], op0=ALU.mult, op1=ALU.add),
                 reads=[kS, bkk(grp, "NUN"), ("E1", d["e3"])], writes=[("ST", hpp, 1 - sq)])
            stq[hpp] = 1 - sq

        def post_gen(d):
            hp_, s_, q = d["hp"], d["s"], d["q2"]
            otm, gbuf, bon = d["otm"], d["gbuf"], d["bon"]
            kO_ = ("OTM", q); kB_ = ("BON", d["g4"]); kG_ = ("G", d["g4"])
            mu, var, rstd = gst[:, 0, :], gst[:, 1, :], gst[:, 2, :]
            t5v = t5[:, :].rearrange("p (c t) -> p c t", t=64)
            P.op("dve", lambda e: e.tensor_reduce(out=mu, in_=otm[:, :, :], axis=AX.X, op=ALU.add), reads=[kO_], writes=["gst0"])
            yield
            P.op("dve", lambda e: e.tensor_scalar(out=mu, in0=mu, scalar1=1.0 / 64, scalar2=None, op0=ALU.mult), reads=["gst0"], writes=["gst0"])
            yield
            P.op("pool", lambda e: e.tensor_tensor(out=otm[:, :, :], in0=otm[:, :, :], in1=mu.unsqueeze(2).broadcast_to([128, NCH, 64]), op=ALU.subtract),
                 reads=[kO_, "gst0"], writes=[kO_])
            yield
            P.op("act", lambda e: e.activation(out=t5v, in_=otm[:, :, :], func=AF.Square), reads=[kO_], writes=["t5"])
            yield
            P.op("dve", lambda e: e.tensor_reduce(out=var, in_=t5v, axis=AX.X, op=ALU.add), reads=["t5"], writes=["gst1"])
            yield
            P.op("dve", lambda e: e.tensor_scalar(out=var, in0=var, scalar1=1.0 / 64, scalar2=64e-5, op0=ALU.mult, op1=ALU.add), reads=["gst1"], writes=["gst1"])
            yield
            P.op("act", lambda e: e.activation(out=var, in_=var, func=AF.Sqrt), reads=["gst1"], writes=["gst1"])
            yield
            P.op("dve", lambda e: e.reciprocal(out=rstd, in_=var), reads=["gst1"], writes=["gst2"])
            yield
            P.op("dve", lambda e: e.tensor_tensor(out=otm[:, :, :], in0=otm[:, :, :], in1=rstd.unsqueeze(2).broadcast_to([128, NCH, 64]), op=ALU.mult),
                 reads=[kO_, "gst2"], writes=[kO_])
            yield
            o4 = otm[:, :, :].unsqueeze(2).broadcast_to([128, NCH, 2, 64])
            P.op("pool", lambda e: e.tensor_tensor(out=v4(OBD[:, :, :]), in0=o4, in1=hm4, op=ALU.mult), reads=[kO_, "cf"], writes=["OBD"])
            yield
            for c4 in range(0, NCH, 4):
                b = self.bank(); kT = ("psb", b); pT = self.ps[b][:, 0:256]
                for q_ in range(4):
                    P.op("pe", lambda e, q_=q_: e.matmul(pT[:, q_ * 64:(q_ + 1) * 64], lhsT=OBD[:, c4 + q_, :], rhs=self.isel_b, start=True, stop=True),
                         reads=["OBD", "cb"], writes=[kT], inc=(q_ == 3), merge=True)
                cs0 = c4 * 64
                P.op("dve", lambda e: e.tensor_scalar(out=t5[:, cs0:cs0 + 256], in0=pT, scalar1=pv(GG, hp_), scalar2=pv(GB, hp_), op0=ALU.mult, op1=ALU.add),
                     reads=[kT, "rv"], writes=["t5"], merge=True)
                yield
                P.op("pool", lambda e: e.tensor_tensor(out=t5[:, cs0:cs0 + 256], in0=t5[:, cs0:cs0 + 256], in1=bon[:, cs0:cs0 + 256], op=ALU.add),
                     reads=["t5", kB_], writes=["t5"], merge=True)
                yield
                ok_, ob = ost.next()
                P.op("pool", lambda e: e.tensor_tensor(out=ob[:, :], in0=t5[:, cs0:cs0 + 256], in1=gbuf[:, cs0:cs0 + 256], op=ALU.mult),
                     reads=["t5", kG_], writes=[ok_])
                P.dma(orT[hp_ * 128:(hp_ + 1) * 128, s_ * SEG + cs0:s_ * SEG + cs0 + 256], ob[:, :], reads=[ok_], writes=["orT"], sem=f"ost{ok_[1]}", merge=True)
                yield

        def prep_gen(d):
            hp, si, kind, s, main, q2 = d["hp"], d["si"], d["kind"], d["s"], d["main"], d["q2"]
            src = ZP if kind == "p" else Z
            skey = "ZP" if kind == "p" else "Z"
            cb0 = 32 + s * SEG
            off = si * SEG
            bdar, bdb, bdk, bdv, e1 = d["bdar"], d["bdb"], d["bdk"], d["bdv"], d["e1"]
            e3i, g4i = d["e3"], d["g4"]
            kq = lambda n: (n, q2)
            if si == 0:
                P.op("pool", lambda e: e.memset(STS[hp % 2][0][:, :], 0.0), writes=[("ST", hp % 2, 0)])
                stq[hp % 2] = 0
            P.dma(zK[:, :], src[c.C_K + hp * 128:c.C_K + (hp + 1) * 128, cb0 - 1:cb0 + SEG], reads=[skey], writes=["zK"], sem="zK")
            yield
            shift(zK, "zK", pv(MU_K, hp), zK[:, 1:W1], "zK")
            yield
            K = zK[:, 1:W1]
            P.dma(zV[:, :], src[c.C_V + hp * 128:c.C_V + (hp + 1) * 128, cb0 - 1:cb0 + SEG], reads=[skey], writes=["zV"], sem="zV")
            yield
            shift(zV, "zV", pv(MU_V, hp), zV[:, 1:W1], "zV")
            yield
            V = zV[:, 1:W1]
            if main:
                P.dma(zR[:, :], src[hp * 128:(hp + 1) * 128, cb0 - 1:cb0 + SEG], reads=[skey], writes=["zR"], sem="zR")
                shift(zR, "zR", pv(MU_R, hp), zR[:, 1:W1], "zR")
            yield
            R = zR[:, 1:W1]
            for tt in range(NT):
                bk, ps = lora_tile(aup_b, "aup", sda, "sda", hp, off, tt, 1)
                P.op("act", lambda e: e.activation(out=AI[:, tt * TW:(tt + 1) * TW], in_=ps, func=AF.Sigmoid, bias=pv(A0, hp)),
                     reads=bk + ["rv"], writes=["AI"], merge=(tt > 0))
            yield
            P.op("pool", lambda e: e.tensor_scalar(out=t2[:, :], in0=K, scalar1=pv(KK, hp), scalar2=None, op0=ALU.mult), reads=["zK", "rv"], writes=["t2"])
            yield
            P.op("act", lambda e: e.activation(out=t3[:, :], in_=t2[:, :], func=AF.Square), reads=["t2"], writes=["t3"])
            yield
            for tt in range(NT):
                bk, ps = headsum_tile(t3, "t3", tt)
                P.op("act", lambda e: e.activation(out=t4[:, tt * TW:(tt + 1) * TW], in_=ps, func=AF.Sqrt), reads=bk, writes=["t4"], merge=(tt > 0))
            yield
            P.op("pool", lambda e: e.tensor_scalar(out=t4[:, :], in0=t4[:, :], scalar1=1e-12, scalar2=None, op0=ALU.max), reads=["t4"], writes=["t4"])
            yield
            P.op("dve", lambda e: e.reciprocal(out=t4[:, :], in_=t4[:, :]), reads=["t4"], writes=["t4"])
            yield
            P.op("pool", lambda e: e.tensor_tensor(out=t2[:, :], in0=t2[:, :], in1=t4[:, :], op=ALU.mult), reads=["t2", "t4"], writes=["t2"])
            yield
            P.op("pool", lambda e: e.tensor_scalar(out=t3[:, :], in0=AI[:, :], scalar1=pv(KA, hp), scalar2=omka[:, hp:hp + 1], op0=ALU.mult, op1=ALU.add),
                 reads=["AI", "rv", "omka"], writes=["t3"])
            yield
            P.op("pool", lambda e: e.tensor_tensor(out=K, in0=K, in1=t3[:, :], op=ALU.mult), reads=["zK", "t3"], writes=["zK"])
            yield
            P.op("pool", lambda e: e.tensor_tensor(out=AI[:, :], in0=AI[:, :], in1=t2[:, :], op=ALU.mult), reads=["AI", "t2"], writes=["AI"])
            yield
            for tt in range(NT):
                bk, ps = lora_tile(wup_b, "wup", tdw, "tdw", hp, off, tt, 1)
                P.op("act", lambda e: e.activation(out=t3[:, tt * TW:(tt + 1) * TW], in_=ps, func=AF.Sigmoid, bias=pv(W0, hp)),
                     reads=bk + ["rv"], writes=["t3"], merge=(tt > 0))
            yield
            P.op("dve", lambda e: e.tensor_tensor_scan(out=t4[:, :], data0=self.reset[:, 0:SEG], data1=t3[:, :], initial=0.0, op0=ALU.mult, op1=ALU.add),
                 reads=["t3", "cf"], writes=["t4"])
            yield
            if main:
                bon = d["bon"]
                P.op("dve", lambda e: e.scalar_tensor_tensor(out=t1[:, :], in0=R, scalar=pv(RK, hp), in1=K, op0=ALU.mult, op1=ALU.mult),
                     reads=["zR", "zK", "rv"], writes=["t1"])
                for tt in range(NT):
                    bk, ps = headsum_tile(t1, "t1", tt)
                    P.op("dve", lambda e: e.tensor_tensor(out=bon[:, tt * TW:(tt + 1) * TW], in0=ps, in1=V[:, tt * TW:(tt + 1) * TW], op=ALU.mult),
                         reads=bk + ["zV"], writes=[("BON", g4i)], merge=(tt > 0))
            yield
            yield
            P.op("act", lambda e: e.activation(out=t1[:, :], in_=t4[:, :], func=AF.Exp, scale=DECAY_C), reads=["t4"], writes=["t1"])
            yield
            P.op("pool", lambda e: e.tensor_tensor(out=K, in0=K, in1=t1[:, :], op=ALU.mult), reads=["zK", "t1"], writes=["zK"])
            yield
            P.op("pool", lambda e: e.tensor_tensor(out=AI[:, :], in0=AI[:, :], in1=t1[:, :], op=ALU.mult), reads=["AI", "t1"], writes=["AI"])
            yield
            bd("pool", K, v4(bdk[:, :, :]), hm4, ["zK"], kq("BDK"))
            yield
            bd("pool", AI[:, :], v4(bdb[:, :, :]), hm4, ["AI"], kq("BDB"))
            yield
            yield
            P.op("pool", lambda e: e.tensor_tensor(out=t1[:, :], in0=t4[:, :], in1=t3[:, :], op=ALU.subtract), reads=["t4", "t3"], writes=["t1"])
            yield
            P.op("act", lambda e: e.activation(out=t1[:, :], in_=t1[:, :], func=AF.Exp, scale=-DECAY_C), reads=["t1"], writes=["t1"])
            yield
            P.op("pool", lambda e: e.tensor_tensor(out=t2[:, :], in0=t2[:, :], in1=t1[:, :], op=ALU.mult), reads=["t2", "t1"], writes=["t2"])
            yield
            bd("pool", t2[:, :], v4(bdar[:, :, 0, :]), hmn4, ["t2"], kq("BDA"))
            yield
            yield
            P.op("act", lambda e: e.activation(out=e1[:, :], in_=t4[:, :], func=AF.Exp, scale=-DECAY_C), reads=["t4"], writes=[("E1", e3i)])
            yield
            bd("pool", V, v4(bdv[:, :, :]), hm4, ["zV"], kq("BDV"))
            yield
            if main:
                P.op("pool", lambda e: e.tensor_tensor(out=R, in0=R, in1=e1[:, :], op=ALU.mult), reads=["zR", ("E1", e3i)], writes=["zR"])
                bd("pool", R, v4(bdar[:, :, 1, :]), hm4, ["zR"], kq("BDR"))
                gbuf = d["gbuf"]
                for tt in range(NT):
                    bk, ps = lora_tile(gup_b, "gup0", sdg, "sdg", hp, s * SEG, tt, 2)
                    P.op("act", lambda e: e.activation(out=gbuf[:, tt * TW:(tt + 1) * TW], in_=ps, func=AF.Copy), reads=bk + ["gup1"], writes=[("G", g4i)], merge=(tt > 0))
            yield

        iters = [(hp, si) for hp in range(NHP) for si in range(len(segs))]
        infos = {}

        def start_prep(j):
            hp, si = iters[j]
            kind, s_ = segs[si]
            q2 = j % 2
            d = dict(hp=hp, si=si, kind=kind, s=s_, main=(kind == "m"), q2=q2, e3=j % 3, g4=j % 4,
                     bdar=BDAR[q2], bdb=BDB[q2], bdk=BDK[q2], bdv=BDV[q2], e1=E1[j % 3], otm=OTM[q2],
                     gbuf=GB_[j % 4], bon=BON[j % 4], groups=[])
            infos[j] = d
            return prep_gen(d)

        def advance(gen, n):
            if gen is None:
                return None
            for _ in range(n):
                try:
                    next(gen)
                except StopIteration:
                    return None
            return gen

        advance(start_prep(0), 10 ** 6)
        for j in range(len(iters)):
            itd = infos[j]
            nxt = start_prep(j + 1) if j + 1 < len(iters) else None
            pst = post_gen(infos[j - 2]) if (j >= 2 and infos[j - 2]["main"]) else None
            for gi, ch0 in enumerate(range(0, NCH, G)):
                grp = dict(itd=itd, x=fctx[gi % 2], y=bctx[bci[0] % 4], ch0=ch0)
                bci[0] += 1
                itd["groups"].append(grp)
            grps = itd["groups"]
            sched = [stage1, stage2, lambda g_: dstep(g_, 1), lambda g_: dstep(g_, 0), lambda g_: dstep(g_, 1),
                     lambda g_: dstep(g_, 0), stage7, stage8, stage9]
            for st_fn in sched:
                for grp in grps:
                    st_fn(grp)
                for _ in range(2):
                    if pend_chain:
                        chain_unit(*pend_chain.pop(0))
                nxt = advance(nxt, 7)
                pst = advance(pst, 4)
            while pend_chain:
                chain_unit(*pend_chain.pop(0))
            advance(nxt, 10 ** 6)
            advance(pst, 10 ** 6)
            for grp in grps:
                for g in range(G):
                    pend_chain.append((grp, g))
        while pend_chain:
            chain_unit(*pend_chain.pop(0))
        for j in (len(iters) - 2, len(iters) - 1):
            if j >= 0 and infos[j]["main"]:
                advance(post_gen(infos[j]), 10 ** 6)


def build(cfg, debug=False):
    B = _build(cfg, debug)
    B.P.es.close()
    return B


def _build(cfg, debug=False):
    c = cfg
    B = Builder(cfg, debug)
    P = B.P
    KT, T, TH, D = c.KT, c.T, c.TH, c.D
    x_main = B.inp("x_main", [T, D]); x_prev = B.inp("x_prev", [T, D]); mem = B.inp("mem_b", [c.MEM, D])
    w_in_t = B.inp("w_in_t", [c.NJIN, 128, KT, 128])
    proj_r_t = B.inp("proj_r_t", [KT, 128, c.DR // 128, 128]); proj_c_t = B.inp("proj_c_t", [KT, 128, c.DC // 128, 128])
    w_out_t = B.inp("w_out_t", [KT, 128, KT, 128]); wq_t = B.inp("wq_t", [KT, 128, KT, 128]); wk_t = B.inp("wk_t", [KT, 128, KT, 128])
    wv_t = B.inp("wv_t", [KT, 128, KT, 128]); wo_t = B.inp("wo_t", [KT, 128, KT, 128])
    w1_t = B.inp("w1_t", [c.DFF // 128, 128, KT, 128]); w2_t = B.inp("w2_t", [c.DFF // D, KT, 128, KT, 128])
    lnv_in = B.inp("lnv", [128, 4, 2, KT]); NCT = c.DC // 128
    convv_in = B.inp("convv", [128, NCT, 32]); clnv_in = B.inp("clnv", [128, 2, NCT])
    out = B.nc.dram_tensor("out", [T, D], F32, kind="ExternalOutput").ap()
    dr = P.dram
    Z = dr("Z", [c.NJIN * 128, TH], F32); ZP = dr("ZP", [c.NJIN * 128, TH], F32)
    tl = lambda nm, ktn, nt: dr(nm, [nt // 128, 128, ktn, 128], F32)
    tv = lambda Yt: (lambda j, m, c0, n: Yt[c0 // 128:(c0 + n) // 128, 0:m, j, :].rearrange("q p t -> p q t"))
    xTf = tl("xTf", KT, T); memT = tl("memT", KT, c.MEM)
    orT = dr("orT", [c.DR, T], BF16); uconv = tl("uconv", c.DC // 128, T); ocT = dr("ocT", [c.DC, T], BF16)
    mr = dr("mr", [D, T], F32); mg = dr("mg", [D, T], BF16); y1 = tl("y1", KT, T); h1 = tl("h1", KT, T)
    qT = dr("qT", [D, T], BF16); oT = dr("oT", [D, T], BF16); y2 = tl("y2", KT, T); h2 = tl("h2", KT, T)
    aT = dr("aT", [c.DFF, T], BF16); y3 = [dr(f"y3_{i}", [D, T], F32) for i in range(c.DFF // D - 1)] + [tl(f"y3_{c.DFF // D - 1}", KT, T)]

    B.setup_consts()
    lnv = P.sbuf("lnv", [128, 4, 2, KT], F32)
    P.dma(lnv[:, :, :, :], lnv_in[:, :, :, :], writes=["gb_ln1", "gb_ln2", "gb_ln3", "gb_lnm"], sem="clnv")

    main_tiles = [(i * c.TT, c.TT) for i in range(T // c.TT)]
    halo_tiles = [(0, 32)] + [(32 + i * c.TT, c.TT) for i in range(T // c.TT)]
    full = lambda j: 128

    with P.scope():
        xt = P.sbuf("xt", [128, KT, TH], BF16)
        for ps_ in ("p", "m"):
            with P.scope():
                B.alloc_ti()
                if ps_ == "p":
                    P.op("pool", lambda e: e.memset(xt[:, :, 0:32], 0.0), writes=["xt"], merge=True)
                    B.transpose_in(x_prev, T, xt, "xt", 32)
                else:
                    B.transpose_in(x_prev[T - 32:T, :], 32, xt, "xt", 0)
                    B.transpose_in(x_main, T, xt, "xt", 32, f32_dst=xTf, tag="xTf")
                    B.transpose_in(mem, c.MEM, None, None, 0, f32_dst=memT, tag="memT")
                    if debug == "p0":
                        P.finish(["xTf_f32", "memT_f32"]); return B
            with P.scope():
                B.alloc_mm()
                if ps_ == "p":
                    jl = list(range(c.C_K // 128, (c.C_DA + 96 + 127) // 128))
                    epi = B.std_epilogue(lambda j, m, c0, n: ZP[j * 128:j * 128 + m, c0:c0 + n], F32, "ZP")
                else:
                    jl = list(range(c.NJIN))
                    epi = B.std_epilogue(lambda j, m, c0, n: Z[j * 128:j * 128 + m, c0:c0 + n], F32, "Z")
                B.mm_phase(xt, ["xt"], KT, halo_tiles, w_in_t, jl, lambda j: min(128, c.DIN - j * 128), epi)
    if debug == "p1":
        P.finish(["Z", "ZP", "xTf_f32", "memT_f32"]); return B
    with P.scope():
        B.rwkv_phase(Z, ZP, orT)
    if debug == "p2":
        P.finish(["orT"]); return B
    with P.scope():
        convv = P.sbuf("convv", [128, NCT, 32], F32)
        P.dma(convv[:, :, :], convv_in[:, :, :], writes=["convw", "convb"], sem="ccv")
        convw = convv[:, :, 0:31]; convb = convv[:, :, 31]
        B._st32 = B.stager("st32c_", 2, [128, 512], F32)
        B.conv_phase(Z, convw, convb, uconv)
    with P.scope():
        clnv = P.sbuf("clnv", [128, 2, NCT], F32)
        P.dma(clnv[:, :, :], clnv_in[:, :, :], writes=["gb_cln"], sem="ccl")
        B.alloc_ln(with_out_bf=True)
        B.ln_phase(uconv, "uconv", NCT, T, clnv, 1, 1e-5, out_bf=ocT, out_bf_key="ocT", silu=True, name="cln")
    if debug == "p3":
        P.finish(["ocT"]); return B
    G0 = c.C_GATE
    with P.scope():
        xt2 = P.sbuf("xt2", [128, KT, T], BF16)
        with P.scope():
            B.alloc_mm()
            B.load_xt(xt2, "xt2", orT, "orT", c.DR // 128, T)
            epi = B.std_epilogue(lambda j, m, c0, n: mr[j * 128:j * 128 + m, c0:c0 + n], F32, "mr",
                                 gate_fn=lambda j, m, c0, n: Z[G0 + j * 128:G0 + j * 128 + m, 32 + c0:32 + c0 + n], gate_key="Z")
            B.mm_phase(xt2, ["xt2"], c.DR // 128, main_tiles, proj_r_t, list(range(KT)), full, epi)
            B.load_xt(xt2, "xt2", ocT, "ocT", c.DC // 128, T)
            epi = B.std_epilogue(lambda j, m, c0, n: mg[j * 128:j * 128 + m, c0:c0 + n], BF16, "mg",
                                 gate_fn=lambda j, m, c0, n: Z[G0 + D + j * 128:G0 + D + j * 128 + m, 32 + c0:32 + c0 + n], gate_key="Z",
                                 res_fn=lambda j, m, c0, n: mr[j * 128:j * 128 + m, c0:c0 + n], res_key="mr")
            B.mm_phase(xt2, ["xt2"], c.DC // 128, main_tiles, proj_c_t, list(range(KT)), full, epi)
            B.load_xt(xt2, "xt2", mg, "mg", KT, T)
            epi = B.std_epilogue(tv(y1), F32, "y1", res_fn=tv(xTf), res_key="xTf_f32", res_scale=c.alpha, out_tiled=True, res_tiled=True)
            B.mm_phase(xt2, ["xt2"], KT, main_tiles, w_out_t, list(range(KT)), full, epi)
        with P.scope():
            B.alloc_ln()
            B.ln_phase(y1, "y1", KT, T, lnv[:, 0, :, :], 0, 1e-5, out_f32=h1, out_f32_key="h1", xt=xt2, xt_key="xt2", name="ln1")
        if debug == "p4":
            P.finish(["h1"]); return B
        with P.scope():
            B.alloc_mm()
            epi = B.std_epilogue(lambda j, m, c0, n: qT[j * 128:j * 128 + m, c0:c0 + n], BF16, "qT")
            B.mm_phase(xt2, ["xt2"], KT, main_tiles, wq_t, list(range(KT)), full, epi)
    with P.scope():
        kT = P.sbuf("kT", [128, KT, c.MEM], BF16)
        vtm = P.sbuf("vtm", [128, c.MEM // 128, D], BF16)
        with P.scope():
            mxt = P.sbuf("mxt", [128, KT, c.MEM], BF16)
            with P.scope():
                B.alloc_ln()
                B.ln_phase(memT, "memT_f32", KT, c.MEM, lnv[:, 3, :, :], 0, 1e-5, xt=mxt, xt_key="mxt", name="lnm")
            with P.scope():
                B.alloc_mm()
                cntk = [0]

                def epi_k(j, m, c0, n, pso, bk):
                    P.op("act", lambda e: e.activation(out=kT[:, j, c0:c0 + n], in_=pso, func=AF.Copy), reads=bk, writes=["kT"], merge=True)

                def epi_v(j, m, c0, n, pso, bk):
                    P.op("dve", lambda e: e.tensor_copy(out=vtm[0:n, c0 // 128, j * 128:j * 128 + m], in_=pso), reads=bk, writes=["vtm"], merge=True)
                B.mm_phase(mxt, ["mxt"], KT, [(0, c.MEM)], wk_t, list(range(KT)), full, epi_k)
                B.mm_phase(mxt, ["mxt"], KT, [(i * 128, 128) for i in range(c.MEM // 128)], wv_t, list(range(KT)), full, epi_v, swap=True)
        with P.scope():
            B._st16 = B.stager("st16a_", 2, [128, 512], BF16)
            B.attn_phase(qT, kT, vtm, oT)
    if debug == "p5":
        P.finish(["oT"]); return B
    with P.scope():
        xt2 = P.sbuf("xt2b", [128, KT, T], BF16)
        with P.scope():
            B.alloc_mm()
            B.load_xt(xt2, "xt2", oT, "oT", KT, T)
            epi = B.std_epilogue(tv(y2), F32, "y2", res_fn=tv(h1), res_key="h1", res_scale=c.alpha, out_tiled=True, res_tiled=True)
            B.mm_phase(xt2, ["xt2"], KT, main_tiles, wo_t, list(range(KT)), full, epi)
        with P.scope():
            B.alloc_ln()
            B.ln_phase(y2, "y2", KT, T, lnv[:, 1, :, :], 0, 1e-5, out_f32=h2, out_f32_key="h2", xt=xt2, xt_key="xt2", name="ln2")
        if debug == "p6":
            P.finish(["h2"]); return B
        with P.scope():
            B.alloc_mm()
            epi = B.std_epilogue(lambda j, m, c0, n: aT[j * 128:j * 128 + m, c0:c0 + n], BF16, "aT", act="relu2")
            B.mm_phase(xt2, ["xt2"], KT, main_tiles, w1_t, list(range(c.DFF // 128)), full, epi)
            for ch in range(c.DFF // D):
                B.load_xt(xt2, "xt2", aT[ch * D:(ch + 1) * D, :], "aT", KT, T)
                pkey = "h2" if ch == 0 else f"y3_{ch - 1}"
                lastc = (ch == c.DFF // D - 1)
                plain = lambda Yp: (lambda j, m, c0, n: Yp[j * 128:j * 128 + m, c0:c0 + n])
                epi = B.std_epilogue(tv(y3[ch]) if lastc else plain(y3[ch]), F32, f"y3_{ch}",
                                     res_fn=(tv(h2) if ch == 0 else plain(y3[ch - 1])), res_key=pkey,
                                     res_scale=(c.alpha if ch == 0 else 1.0), out_tiled=lastc, res_tiled=(ch == 0))
                B.mm_phase(xt2, ["xt2"], KT, main_tiles, w2_t[ch], list(range(KT)), full, epi)
    with P.scope():
        B.alloc_ln()
        B._xin = Ring("xino", [P.sbuf(f"xino{i}", [128, D], F32) for i in range(2)])
        last = c.DFF // D - 1
        B.ln_phase(y3[last], f"y3_{last}", KT, T, lnv[:, 2, :, :], 0, 1e-5, final_out=out, name="ln3")
    P.finish(["final"])
    return B


def make_consts():
    cst = np.zeros((128, 1792), np.float32)
    p = np.arange(128)
    h = p // 64; t = p % 64
    same = (h[:, None] == h[None, :]).astype(np.float32)
    cst[:, 0:128] = np.eye(128)
    cst[:, 128:256] = same
    cst[:, 256:384] = same * (t[None, :] < t[:, None])
    cst[:, 384:512] = same * (t[None, :] > t[:, None])
    cst[:, 512:640] = same * (t[None, :] >= t[:, None])
    cst[:, 640:704] = (t[:, None] == np.arange(64)[None, :])
    cst[:, 704] = (h == 0); cst[:, 705] = (h == 1)
    cst[:, 706] = -(h == 0).astype(np.float32); cst[:, 707] = -(h == 1).astype(np.float32)
    cst[:, 708:708 + 1024] = (np.arange(1024) % 64 != 0).astype(np.float32)[None, :]
    return cst


def tile_w(w, pad_to=None):
    K, N = w.shape
    NJ = (N + 127) // 128
    if NJ * 128 != N:
        w = np.concatenate([w, np.zeros((K, NJ * 128 - N), w.dtype)], axis=1)
    return np.ascontiguousarray(w.reshape(K // 128, 128, NJ, 128).transpose(2, 1, 0, 3))


def pvec(v):
    return np.ascontiguousarray(v.reshape(-1, 128).T)


def prepare_inputs(cfg, inputs):
    c = cfg
    f = lambda k: np.asarray(inputs[k], dtype=np.float32)
    sq = lambda k: f(k)[0]
    shared = {}
    shared["consts"] = make_consts()
    shared["w_in_t"] = tile_w(sq("w_in"))
    shared["proj_r_t"] = tile_w(sq("proj_rwkv")); shared["proj_c_t"] = tile_w(sq("proj_conv"))
    shared["w_out_t"] = tile_w(sq("w_out")); shared["wq_t"] = tile_w(sq("xattn_wq")); shared["wk_t"] = tile_w(sq("xattn_wk"))
    shared["wv_t"] = tile_w(sq("xattn_wv")); shared["wo_t"] = tile_w(sq("xattn_wo"))
    shared["w1_t"] = tile_w(sq("mlp_w1"))
    w2 = sq("mlp_w2")
    shared["w2_t"] = np.stack([tile_w(w2[i * c.D:(i + 1) * c.D]) for i in range(c.DFF // c.D)])
    lnv = np.zeros((128, 4, 2, c.KT), np.float32)
    for i, (g, b) in enumerate([("ln1_g", "ln1_b"), ("ln2_g", "ln2_b"), ("ln3_g", "ln3_b")]):
        lnv[:, i, 0] = pvec(sq(g)); lnv[:, i, 1] = pvec(sq(b))
    lnv[:, 3, 0] = pvec(f("ln_mem_g")); lnv[:, 3, 1] = pvec(f("ln_mem_b"))
    shared["lnv"] = lnv
    NCT = c.DC // 128
    cw = sq("conv_w")[:, 0, :]
    convv = np.zeros((128, NCT, 32), np.float32)
    convv[:, :, 0:31] = cw.T.reshape(NCT, 128, 31).transpose(1, 0, 2)
    convv[:, :, 31] = pvec(sq("conv_b"))
    shared["convv"] = convv
    clnv = np.zeros((128, 2, NCT), np.float32)
    clnv[:, 0] = pvec(sq("conv_ln_g")); clnv[:, 1] = pvec(sq("conv_ln_b"))
    shared["clnv"] = clnv
    NHP = c.NHP
    sm = sq("rwkv_shift_mix")
    rvec = np.zeros((128, 10 * NHP + 4), np.float32)
    cols = [sm[0:c.DR], sm[c.DR:2 * c.DR], sm[2 * c.DR:3 * c.DR], sq("rwkv_w0"), sq("rwkv_a0"), sq("rwkv_k_k"), sq("rwkv_k_a"),
            sq("rwkv_r_k").reshape(-1), sq("rwkv_gn_g"), sq("rwkv_gn_b")]
    for i, v in enumerate(cols):
        rvec[:, i * NHP:(i + 1) * NHP] = pvec(v)
    rvec[0:96, 10 * NHP] = sm[c.C_DW:c.C_DW + 96]; rvec[0:96, 10 * NHP + 1] = sm[c.C_DA:c.C_DA + 96]
    rvec[:, 10 * NHP + 2:10 * NHP + 4] = pvec(sm[c.C_DG:c.C_DG + 256])
    shared["rvec"] = rvec
    wu = np.zeros((128, c.DR), np.float32); wu[0:96] = sq("rwkv_w_up"); shared["w_up_t"] = wu
    au = np.zeros((128, c.DR), np.float32); au[0:96] = sq("rwkv_a_up"); shared["a_up_t"] = au
    shared["g_up_t"] = np.ascontiguousarray(sq("rwkv_g_up").reshape(2, 128, c.DR).transpose(1, 0, 2))
    x = f("x"); mem = f("mem")
    in_maps = []
    for core in range(c.NCORES):
        b, sh = core // 2, core % 2
        m = dict(shared)
        m["x_main"] = np.ascontiguousarray(x[b, sh * c.T:(sh + 1) * c.T])
        m["x_prev"] = np.ascontiguousarray(x[b, 0:c.T]) if sh == 1 else np.zeros((c.T, c.D), np.float32)
        m["mem_b"] = np.ascontiguousarray(mem[b])
        in_maps.append(m)
    return in_maps


_CACHE = {}


def kernel(**inputs):
    cfg = Cfg()
    if "B" not in _CACHE:
        _CACHE["B"] = build(cfg)
    B = _CACHE["B"]
    in_maps = prepare_inputs(cfg, inputs)
    res = run_bass_kernel_spmd(B.nc, in_maps, core_ids=list(range(cfg.NCORES)))
    out = np.zeros((cfg.BATCH, cfg.SEQ, cfg.D), np.float32)
    for core in range(cfg.NCORES):
        b, sh = core // 2, core % 2
        out[b, sh * cfg.T:(sh + 1) * cfg.T] = res.results[core]["out"]
    return out
```

```python
import contextlib
import numpy as np
import concourse.bass as bass
import concourse.mybir as mybir
from concourse.bass_utils import run_bass_kernel_spmd

F32 = mybir.dt.float32
BF16 = mybir.dt.bfloat16
AF = mybir.ActivationFunctionType
ALU = mybir.AluOpType
AX = mybir.AxisListType
DECAY_C = 0.6065306597126334


class Cfg:
    def __init__(s, D=4096, SEQ=4096, BATCH=4, MEM=256, XH=4):
        s.D = D; s.SEQ = SEQ; s.BATCH = BATCH; s.MEM = MEM; s.XH = XH
        s.DR = D // 2; s.NHP = s.DR // 128
        s.RW = 96; s.RA = 96; s.RG = 256
        s.DC = D // 2; s.CW = 31
        s.DFF = 4 * D
        s.T = SEQ // 2; s.HALO = 32; s.TH = s.T + 32
        s.KT = D // 128
        s.NRW = 3 * s.DR + s.RW + s.RA + s.RG
        s.DIN = s.NRW + 2 * s.DC + 2 * D
        s.NJIN = (s.DIN + 127) // 128
        s.C_K = s.DR; s.C_V = 2 * s.DR; s.C_DW = 3 * s.DR; s.C_DA = s.C_DW + 96; s.C_DG = s.C_DA + 96
        s.C_CONV = s.NRW; s.C_GATE = s.NRW + 2 * s.DC
        s.alpha = float(2.0 ** 0.25)
        s.NCORES = 2 * BATCH
        s.SEG = min(512, s.T)
        s.TT = 512
        s.XD = D // XH


class Prog:
    ENG = ("pe", "act", "dve", "pool", "sp")

    def __init__(self, nc, debug=False):
        self.nc = nc
        self.debug = debug
        self.es = contextlib.ExitStack()
        self.eng = dict(pe=nc.tensor, act=nc.scalar, dve=nc.vector, pool=nc.gpsimd, sp=nc.sync)
        self.sems = []
        self.psem = {}
        for e in self.ENG:
            self.psem[e] = self.new_sem("p_" + e)
        self.pcnt = {e: 0 for e in self.ENG}
        self.seen = {e: {} for e in self.ENG}
        self.lastw = {}
        self.lastr = {}
        self.dsem = {}
        self.n_inst = 0
        self.dram_out = {}

    def new_sem(self, name):
        s = self.es.enter_context(self.nc.semaphore(name))
        self.sems.append(s)
        return len(self.sems) - 1

    def sbuf(self, name, shape, dt):
        self._nalloc = getattr(self, "_nalloc", 0) + 1
        st = self.scopes[-1] if getattr(self, "scopes", None) else self.es
        return st.enter_context(self.nc.sbuf_tensor(f"{name}_{self._nalloc}", list(shape), dt))

    @contextlib.contextmanager
    def scope(self):
        if not hasattr(self, "scopes"):
            self.scopes = []
        st = contextlib.ExitStack()
        self.scopes.append(st)
        try:
            yield
        finally:
            self.barrier()
            self.scopes.pop()
            st.close()

    def barrier(self):
        toks = [(self.psem[e], self.pcnt[e], e) for e in self.ENG if self.pcnt[e] > 0]
        toks += [(ent[0], ent[1], "dma") for ent in self.dsem.values()]
        for e in self.ENG:
            self._wait(e, toks)

    def psum(self, name, shape, dt):
        return self.es.enter_context(self.nc.psum_tensor(name, list(shape), dt))

    def dram(self, name, shape, dt):
        kind = "ExternalOutput" if self.debug else "Internal"
        t = self.nc.dram_tensor(name, list(shape), dt, kind=kind)
        return t.ap()

    def _wait(self, e, toks, raw=()):
        need = {}
        for (si, val, te) in toks:
            if te == e:
                continue
            if val > need.get(si, 0):
                need[si] = val
        for (si, val, te) in raw:
            if val > need.get(si, 0):
                need[si] = val
        for si, val in need.items():
            if self.seen[e].get(si, 0) < val:
                self.eng[e].wait_ge(self.sems[si], val)
                self.seen[e][si] = val

    def _raw(self, reads):
        toks = []
        for r in reads:
            toks.extend(self.lastw.get(r, {}).values())
        return toks

    def _deps(self, reads, writes, merge=False):
        toks = []
        for w in writes:
            if not merge:
                toks.extend(self.lastw.get(w, {}).values())
            toks.extend(self.lastr.get(w, {}).values())
        return toks

    def _record(self, tok, reads, writes, merge):
        si = tok[0]
        for w in writes:
            if merge:
                self.lastw.setdefault(w, {})[si] = tok
            else:
                self.lastw[w] = {si: tok}
                self.lastr[w] = {}
        for r in reads:
            self.lastr.setdefault(r, {})[si] = tok

    @staticmethod
    def _is_ps(k):
        return isinstance(k, tuple) and len(k) > 0 and k[0] == "psb"

    def op(self, e, fn, reads=(), writes=(), inc=True, merge=False):
        pk = [k for k in list(reads) + list(writes) if self._is_ps(k)]
        reads = [k for k in reads if not self._is_ps(k)]
        writes = [k for k in writes if not self._is_ps(k)]
        toks = self._deps(reads, writes, merge)
        for k in pk:
            toks.extend(self.lastw.get(k, {}).values())
        self._wait(e, toks, self._raw(reads))
        inst = fn(self.eng[e])
        self.n_inst += 1
        if inc:
            self.pcnt[e] += 1
            inst.then_inc(self.sems[self.psem[e]], 1)
            tok = (self.psem[e], self.pcnt[e], e)
        else:
            tok = (self.psem[e], self.pcnt[e] + 1, e)
        self._record(tok, reads, writes, merge)
        for k in pk:
            self.lastw.setdefault(k, {})[tok[0]] = tok
        return inst

    def dma(self, out, in_, reads=(), writes=(), sem="d0", q="sp", merge=False, **kw):
        self._wait(q, self._deps(reads, writes, merge), self._raw(reads))
        if sem not in self.dsem:
            self.dsem[sem] = [self.new_sem("d_" + sem), 0]
        ent = self.dsem[sem]
        inst = self.eng[q].dma_start(out=out, in_=in_, **kw)
        ent[1] += 16
        inst.then_inc(self.sems[ent[0]], 16)
        self.n_inst += 1
        tok = (ent[0], ent[1], "dma")
        self._record(tok, reads, writes, merge)

    def finish(self, keys):
        toks = []
        for k in keys:
            toks.extend(self.lastw.get(k, {}).values())
        self._wait("sp", toks)


class Ring:
    def __init__(self, name, items):
        self.name = name; self.items = items; self.i = 0

    def next(self):
        k = self.i % len(self.items)
        self.i += 1
        return (self.name, k), self.items[k]


class Builder:
    def __init__(self, cfg, debug=False):
        self.c = cfg
        self.debug = debug
        self.nc = bass.Bass("TRN2", target_bir_lowering=False)
        self.P = Prog(self.nc, debug)
        self.inputs = {}

    def inp(self, name, shape, dt=F32):
        t = self.nc.dram_tensor(name, list(shape), dt, kind="ExternalInput").ap()
        self.inputs[name] = t
        return t

    def setup_consts(self):
        P, c = self.P, self.c
        cin = self.inp("consts", [128, 1792])
        self.cf = P.sbuf("cf", [128, 1792], F32)
        P.dma(self.cf[:, :], cin[:, :], writes=["cf"], sem="ccf")
        cf = self.cf
        self.ident_f = cf[:, 0:128]
        self.bones_f = cf[:, 128:256]
        self.hmask = cf[:, 704:706]
        self.hmaskn = cf[:, 706:708]
        self.reset = cf[:, 708:708 + 1024]
        self.cb = P.sbuf("cb", [128, 704], BF16)
        P.op("dve", lambda e: e.tensor_copy(out=self.cb[:, :], in_=cf[:, 0:704]), reads=["cf"], writes=["cb"])
        cb = self.cb
        self.ident_b = cb[:, 0:128]
        self.mask_sl = cf[:, 256:384]
        self.mask_su_u = cf[:, 384:640]
        self.mask_su = cf[:, 384:512]
        self.isel_b = cb[:, 640:704]
        self.ones_b = P.sbuf("ones_b", [128, 2, 128], BF16)
        P.op("pool", lambda e: e.memset(self.ones_b[:, 0, :], 1.0 / c.D), writes=["ones0"])
        P.op("pool", lambda e: e.memset(self.ones_b[:, 1, :], 1.0 / c.DC), writes=["ones1"])
        self.ps = [P.psum(f"ps{i}", [128, 512], F32) for i in range(8)]
        self.ps_i = 0
        self.psh_i = 0

    def bank(self):
        b = self.ps_i % 8
        self.ps_i += 1
        return b

    def half(self):
        b = self.bank()
        return ("psb", b), self.ps[b][:, 0:256]

    def bank_keys(self, b):
        return [("psb", b)]

    def alloc_ti(self, with_f32=True):
        P, c = self.P, self.c
        self._xin = Ring("xin", [P.sbuf(f"xin{i}", [128, c.D], F32) for i in range(2)])
        if with_f32:
            self._xf = Ring("xf", [P.sbuf(f"xf{i}", [128, c.KT, 128], F32) for i in range(2)])

    def transpose_in(self, src, ntok, xt, xt_key, col0, f32_dst=None, tag="ti"):
        P, c = self.P, self.c
        KT = c.KT
        xin = self._xin
        nblk = (ntok + 127) // 128
        for i in range(nblk):
            nt = min(128, ntok - i * 128)
            xk, xb = xin.next()
            P.dma(xb[0:nt, :], src[i * 128:i * 128 + nt, :], writes=[xk], sem=f"xin{xk[1]}")
            if f32_dst is not None:
                fk, fb = self._xf.next()
            for g in range(0, KT, 4):
                gn = min(4, KT - g)
                b = self.bank()
                bk = self.bank_keys(b)
                for q in range(gn):
                    P.op("pe", lambda e, q=q: e.transpose(self.ps[b][:, q * 128:q * 128 + nt],
                                                          xb[0:nt, (g + q) * 128:(g + q + 1) * 128],
                                                          self.ident_f[0:nt, 0:nt]),
                         reads=[xk, "cf"], writes=bk, inc=(q == gn - 1), merge=(q > 0))
                src_ps = self.ps[b][:, 0:gn * 128].rearrange("p (a t) -> p a t", a=gn)[:, :, 0:nt]
                eng = "act" if (g // 4) % 2 == 0 else "dve"
                if xt is not None:
                    dst = xt[:, g:g + gn, col0 + i * 128:col0 + i * 128 + nt]
                    if eng == "act":
                        P.op("act", lambda e: e.activation(out=dst, in_=src_ps, func=AF.Copy), reads=bk, writes=[xt_key], merge=True)
                    else:
                        P.op("dve", lambda e: e.tensor_copy(out=dst, in_=src_ps), reads=bk, writes=[xt_key], merge=True)
                if f32_dst is not None:
                    eng2 = "dve" if eng == "act" else "act"
                    dst2 = fb[:, g:g + gn, 0:nt]
                    if eng2 == "act":
                        P.op("act", lambda e: e.activation(out=dst2, in_=src_ps, func=AF.Copy), reads=bk, writes=[fk], merge=(g > 0))
                    else:
                        P.op("dve", lambda e: e.tensor_copy(out=dst2, in_=src_ps), reads=bk, writes=[fk], merge=(g > 0))
            if f32_dst is not None:
                P.dma(f32_dst[i][:, :, 0:nt], fb[:, :, 0:nt], reads=[fk], writes=[tag + "_f32"], sem=f"xf{fk[1]}", merge=True)

    def alloc_mm(self):
        P, c = self.P, self.c
        KH = (c.KT + 1) // 2
        self._wst = Ring("wst", [P.sbuf(f"wst{i}", [128, KH, 128], F32) for i in range(3)])
        self._wbf = Ring("wbf", [P.sbuf(f"wbf{i}", [128, c.KT, 128], BF16) for i in range(2)])
        self._st32 = self.stager("st32_", 2, [128, 512], F32)
        self._st16 = self.stager("st16_", 2, [128, 512], BF16)
        self._lda = self.stager("lda_", 4, [128, 512], F32)
        self._ldb = self.stager("ldb_", 4, [128, 512], F32)
        self._tmp = self.stager("tmp_", 2, [128, 512], F32)

    def mm_phase(self, xt, xt_keys, KT, tok_tiles, Wt, jlist, msz, epi, swap=False):
        P = self.P
        halves = [(0, (KT + 1) // 2), ((KT + 1) // 2, KT)] if KT > 1 else [(0, 1)]
        items = [(idx, h) for idx in range(len(jlist)) for h in range(len(halves))]
        PF = 3
        loaded = {}

        def load(ii):
            if ii >= len(items):
                return
            idx, h = items[ii]
            k0, k1 = halves[h]
            sk, sb = self._wst.next()
            P.dma(sb[:, 0:k1 - k0, :], Wt[jlist[idx]][:, k0:k1, :], writes=[sk], sem=f"wst{sk[1]}")
            loaded[ii] = (sk, sb)

        for ii in range(min(PF, len(items))):
            load(ii)
        ii = 0
        pfn = getattr(epi, "prefetch", None)
        tiles = [(j, msz(j), c0, n) for j in jlist for (c0, n) in tok_tiles]
        PFE = 2
        if pfn is not None:
            for t_ in tiles[:PFE]:
                pfn(*t_)
        tcount = 0
        for idx, j in enumerate(jlist):
            m = msz(j)
            wk, wb = self._wbf.next()
            for h, (k0, k1) in enumerate(halves):
                sk, sb = loaded.pop(ii)
                P.op("pool", lambda e: e.tensor_copy(out=wb[:, k0:k1, :], in_=sb[:, 0:k1 - k0, :]), reads=[sk], writes=[wk], merge=(h > 0))
                load(ii + PF)
                ii += 1
            for (c0, n) in tok_tiles:
                b = self.bank()
                bk = self.bank_keys(b)
                if not swap:
                    pso = self.ps[b][0:m, 0:n]
                else:
                    pso = self.ps[b][0:n, 0:m]
                for k in range(KT):
                    if not swap:
                        fn = lambda e, k=k: e.matmul(pso, lhsT=wb[:, k, 0:m], rhs=xt[:, k, c0:c0 + n], start=(k == 0), stop=(k == KT - 1))
                    else:
                        fn = lambda e, k=k: e.matmul(pso, lhsT=xt[:, k, c0:c0 + n], rhs=wb[:, k, 0:m], start=(k == 0), stop=(k == KT - 1))
                    P.op("pe", fn, reads=[wk] + (list(xt_keys) if k == 0 else []), writes=bk, inc=(k == KT - 1), merge=(k > 0))
                if pfn is not None and tcount + PFE < len(tiles):
                    pfn(*tiles[tcount + PFE])
                tcount += 1
                epi(j, m, c0, n, pso, bk)

    def stager(self, name, n, shape, dt):
        return Ring(name, [self.P.sbuf(f"{name}{i}", shape, dt) for i in range(n)])

    def std_epilogue(self, out_fn, out_dt, out_key, gate_fn=None, res_fn=None, res_scale=1.0, res_key=None, gate_key=None,
                     act=None, out_tiled=False, res_tiled=False):
        P = self
        PP = self.P
        nm = f"ep{getattr(self, '_epn', 0)}"
        self._epn = getattr(self, '_epn', 0) + 1
        cnt = [0]
        pre = {}

        def prefetch(j, m, c0, n):
            gk = gb = rk = rb = None
            if gate_fn is not None:
                gk, gb = self._lda.next()
                PP.dma(gb[0:m, 0:n], gate_fn(j, m, c0, n), reads=[gate_key], writes=[gk], sem=f"lda{gk[1]}")
            if res_fn is not None:
                rk, rb = self._ldb.next()
                rdst = rb[0:m, 0:n].rearrange("p (q t) -> p q t", t=128) if res_tiled else rb[0:m, 0:n]
                PP.dma(rdst, res_fn(j, m, c0, n), reads=[res_key], writes=[rk], sem=f"ldb{rk[1]}")
            pre[(j, c0)] = (gk, gb, rk, rb)

        def epi(j, m, c0, n, pso, bk):
            i = cnt[0]; cnt[0] += 1
            sk, sb = (self._st32 if out_dt == F32 else self._st16).next()
            so = sb[0:m, 0:n]
            if (j, c0) not in pre:
                prefetch(j, m, c0, n)
            gk, gb, rk, rb = pre.pop((j, c0))
            if gate_fn is not None:
                PP.op("act", lambda e: e.activation(out=gb[0:m, 0:n], in_=gb[0:m, 0:n], func=AF.Sigmoid), reads=[gk], writes=[gk])
                if res_fn is None:
                    PP.op("dve", lambda e: e.tensor_tensor(out=so, in0=pso, in1=gb[0:m, 0:n], op=ALU.mult), reads=bk + [gk], writes=[sk])
                else:
                    PP.op("dve", lambda e: e.tensor_tensor(out=gb[0:m, 0:n], in0=pso, in1=gb[0:m, 0:n], op=ALU.mult), reads=bk + [gk], writes=[gk])
                    PP.op("pool", lambda e: e.tensor_tensor(out=so, in0=gb[0:m, 0:n], in1=rb[0:m, 0:n], op=ALU.add), reads=[gk, rk], writes=[sk])
            elif res_fn is not None:
                PP.op("dve", lambda e: e.scalar_tensor_tensor(out=so, in0=rb[0:m, 0:n], scalar=float(res_scale), in1=pso,
                                                              op0=ALU.mult, op1=ALU.add), reads=bk + [rk], writes=[sk])
            elif act == "relu2":
                tk, tb = self._tmp.next()
                PP.op("dve", lambda e: e.tensor_scalar(out=tb[0:m, 0:n], in0=pso, scalar1=0.0, scalar2=None, op0=ALU.max), reads=bk, writes=[tk])
                PP.op("act", lambda e: e.activation(out=so, in_=tb[0:m, 0:n], func=AF.Square), reads=[tk], writes=[sk])
            else:
                if i % 2 == 0:
                    PP.op("act", lambda e: e.activation(out=so, in_=pso, func=AF.Copy), reads=bk, writes=[sk])
                else:
                    PP.op("dve", lambda e: e.tensor_copy(out=so, in_=pso), reads=bk, writes=[sk])
            so_d = so.rearrange("p (q t) -> p q t", t=128) if out_tiled else so
            PP.dma(out_fn(j, m, c0, n), so_d, reads=[sk], writes=[out_key], sem=f"{sk[0]}{sk[1]}", merge=True)
        epi.prefetch = prefetch if (gate_fn is not None or res_fn is not None) else None
        return epi

    def load_xt(self, xt, key, src, src_key, KT, ncol, col0=0):
        P = self.P
        srcv = src.rearrange("(k p) t -> p k t", p=128)
        step = max(1, KT // 4)
        first = True
        for i, k0 in enumerate(range(0, KT, step)):
            k1 = min(KT, k0 + step)
            P.dma(xt[:, k0:k1, col0:col0 + ncol], srcv[:, k0:k1, 0:ncol], reads=[src_key], writes=[key], sem=f"xl{i % 4}", merge=not first)
            first = False

    def alloc_ln(self, with_out_bf=False):
        P, c = self.P, self.c
        KTM = c.KT
        self._lny = Ring("lny", [P.sbuf(f"lny{i}", [128, KTM, 128], F32) for i in range(2)])
        self._lnb = Ring("lnb", [P.sbuf(f"lnb{i}", [128, 2, KTM, 128], BF16) for i in range(2)])
        self._lns = Ring("lns", [P.sbuf(f"lns{i}", [128, 4, 128], F32) for i in range(2)])
        if with_out_bf:
            self._lno = Ring("lno", [P.sbuf(f"lno{i}", [128, KTM, 128], BF16) for i in range(2)])

    def ln_phase(self, src, src_key, KTn, ntok, gb, ones_idx, eps, out_f32=None, out_f32_key=None, xt=None, xt_key=None,
                 out_bf=None, out_bf_key=None, silu=False, final_out=None, name="ln"):
        P, c = self.P, self.c
        KTM = c.KT
        ones = self.ones_b[:, ones_idx, :]
        def partA(ti):
            c0 = ti * 128
            yk, yb = self._lny.next()
            Y = yb[:, 0:KTn, :]
            P.dma(Y, src[ti], reads=[src_key], writes=[yk], sem=f"lny{yk[1]}")
            bk_, bb = self._lnb.next()
            P.op("pool", lambda e: e.tensor_copy(out=bb[:, 0, 0:KTn, :], in_=Y), reads=[yk], writes=[(bk_, 0)])
            P.op("act", lambda e: e.activation(out=bb[:, 1, 0:KTn, :], in_=Y, func=AF.Square), reads=[yk], writes=[(bk_, 1)])
            pk = []
            for s in range(2):
                hk, hp = self.half()
                for k in range(KTn):
                    P.op("pe", lambda e, k=k: e.matmul(hp[:, 0:128], lhsT=ones, rhs=bb[:, s, k, :], start=(k == 0), stop=(k == KTn - 1)),
                         reads=[(bk_, s), f"ones{ones_idx}"], writes=[hk], inc=(k == KTn - 1), merge=(k > 0))
                pk.append((hk, hp[:, 0:128]))
            return dict(yk=yk, yb=yb, Y=Y, pk=pk)

        def partB(ti, hnd):
            c0 = ti * 128
            yk, yb, Y, pk = hnd["yk"], hnd["yb"], hnd["Y"], hnd["pk"]
            sk, st = self._lns.next()
            mean, msq = st[:, 0, :], st[:, 1, :]
            kR, pR = self.half()
            rstd, m2 = pR[:, 0:128], pR[:, 128:256]
            P.op("act", lambda e: e.activation(out=mean, in_=pk[0][1], func=AF.Copy), reads=[pk[0][0]], writes=[(sk, 0)])
            P.op("dve", lambda e: e.tensor_tensor(out=msq, in0=mean, in1=mean, op=ALU.mult), reads=[(sk, 0)], writes=[(sk, 1)])
            P.op("dve", lambda e: e.tensor_tensor(out=msq, in0=pk[1][1], in1=msq, op=ALU.subtract), reads=[pk[1][0], (sk, 1)], writes=[(sk, 1)])
            P.op("dve", lambda e: e.tensor_scalar(out=msq, in0=msq, scalar1=float(eps), scalar2=None, op0=ALU.add), reads=[(sk, 1)], writes=[(sk, 1)])
            P.op("act", lambda e: e.activation(out=msq, in_=msq, func=AF.Sqrt), reads=[(sk, 1)], writes=[(sk, 1)])
            P.op("dve", lambda e: e.reciprocal(out=rstd, in_=msq), reads=[(sk, 1)], writes=[kR])
            P.op("dve", lambda e: e.tensor_tensor(out=m2, in0=mean, in1=rstd, op=ALU.mult), reads=[(sk, 0), kR], writes=[kR])
            rb = rstd.unsqueeze(1).broadcast_to([128, KTn, 128])
            mb = m2.unsqueeze(1).broadcast_to([128, KTn, 128])
            P.op("dve", lambda e: e.tensor_tensor(out=Y, in0=Y, in1=rb, op=ALU.mult), reads=[yk, kR], writes=[yk])
            P.op("dve", lambda e: e.tensor_tensor(out=Y, in0=Y, in1=mb, op=ALU.subtract), reads=[yk, kR], writes=[yk])
            fn_ = AF.Silu if silu else AF.Identity
            need_f32 = (out_f32 is not None) or (final_out is not None)
            if out_bf is not None:
                ok, ob = self._lno.next()
            for k in range(KTn):
                if need_f32:
                    dstk = Y[:, k, :]; wk_ = [yk]
                elif xt is not None:
                    dstk = xt[:, k, c0:c0 + 128]; wk_ = [xt_key]
                else:
                    dstk = ob[:, k, :]; wk_ = [ok]
                P.op("act", lambda e, k=k: e.activation(out=dstk, in_=Y[:, k, :], func=fn_, scale=gb[:, 0, k:k + 1], bias=gb[:, 1, k:k + 1]),
                     reads=[yk, "gb_" + name], writes=wk_, merge=True)
            if out_f32 is not None:
                P.dma(out_f32[ti], Y, reads=[yk], writes=[out_f32_key], sem=f"lnyo{yk[1]}", merge=True)
            if need_f32 and xt is not None:
                P.op("pool", lambda e: e.tensor_copy(out=xt[:, 0:KTn, c0:c0 + 128], in_=Y), reads=[yk], writes=[xt_key], merge=True)
            if need_f32 and out_bf is not None:
                P.op("pool", lambda e: e.tensor_copy(out=ob[:, 0:KTn, :], in_=Y), reads=[yk], writes=[ok])
            if out_bf is not None:
                P.dma(out_bf.rearrange("(k p) t -> p k t", p=128)[:, :, c0:c0 + 128], ob[:, 0:KTn, :], reads=[ok], writes=[out_bf_key],
                      sem=f"lno{ok[1]}", merge=True)
            if final_out is not None:
                xk, xb = self._xin.next()
                for g in range(0, KTn, 4):
                    b = self.bank()
                    bk = self.bank_keys(b)
                    for q in range(4):
                        P.op("pe", lambda e, q=q: e.transpose(self.ps[b][:, q * 128:(q + 1) * 128], yb[:, g + q, :], self.ident_f),
                             reads=[yk, "cf"], writes=bk, inc=(q == 3), merge=(q > 0))
                    if (g // 4) % 2 == 0:
                        P.op("act", lambda e: e.activation(out=xb[:, g * 128:(g + 4) * 128], in_=self.ps[b][:, :], func=AF.Copy), reads=bk, writes=[xk], merge=(g > 0))
                    else:
                        P.op("dve", lambda e: e.tensor_copy(out=xb[:, g * 128:(g + 4) * 128], in_=self.ps[b][:, :]), reads=bk, writes=[xk], merge=(g > 0))
                P.dma(final_out[c0:c0 + 128, :], xb[:, :], reads=[xk], writes=["final"], sem=f"xin{xk[1]}", merge=True)

        nt_ = ntok // 128
        if final_out is not None:
            for ti in range(nt_):
                partB(ti, partA(ti))
        else:
            hnds = {0: partA(0)}
            for ti in range(nt_):
                if ti + 1 < nt_:
                    hnds[ti + 1] = partA(ti + 1)
                partB(ti, hnds.pop(ti))


    def conv_phase(self, Z, convw, convb, uout):
        P, c = self.P, self.c
        T, TH = c.T, c.TH
        NCT = c.DC // 128
        za = Ring("cva", [P.sbuf(f"cva{i}", [128, TH], F32) for i in range(2)])
        zg = Ring("cvg", [P.sbuf(f"cvg{i}", [128, TH], F32) for i in range(2)])
        ub = Ring("cvu", [P.sbuf(f"cvu{i}", [128, TH], BF16) for i in range(2)])
        dg = Ring("cvd", [P.sbuf(f"cvd{i}", [128, c.CW, 128], BF16) for i in range(2)])
        for ct in range(NCT):
            ak, ab = za.next(); gk, gbf = zg.next(); uk, ubf = ub.next(); dk, dgt = dg.next()
            r0 = c.C_CONV + ct * 128
            P.dma(ab[:, :], Z[r0:r0 + 128, :], reads=["Z"], writes=[ak], sem=f"cva{ak[1]}")
            P.dma(gbf[:, :], Z[r0 + c.DC:r0 + c.DC + 128, :], reads=["Z"], writes=[gk], sem=f"cvg{gk[1]}")
            P.op("act", lambda e: e.activation(out=gbf[:, :], in_=gbf[:, :], func=AF.Sigmoid), reads=[gk], writes=[gk])
            P.op("dve", lambda e: e.tensor_tensor(out=ubf[:, :], in0=ab[:, :], in1=gbf[:, :], op=ALU.mult), reads=[ak, gk], writes=[uk])
            for j in range(c.CW):
                P.op("pool", lambda e, j=j: e.tensor_scalar(out=dgt[:, j, :], in0=self.ident_f, scalar1=convw[:, ct, j:j + 1], scalar2=None, op0=ALU.mult),
                     reads=["cf", "convw"], writes=[dk], merge=(j > 0))
            for tt in range(T // c.TT):
                b = self.bank(); bk = self.bank_keys(b)
                for j in range(c.CW):
                    cs = 2 + j + tt * c.TT
                    P.op("pe", lambda e, j=j, cs=cs: e.matmul(self.ps[b][:, :], lhsT=dgt[:, j, :], rhs=ubf[:, cs:cs + c.TT],
                                                             start=(j == 0), stop=(j == c.CW - 1)),
                         reads=[dk, uk], writes=bk, inc=(j == c.CW - 1), merge=(j > 0))
                sk, sb = self._st32.next()
                P.op("act", lambda e: e.activation(out=sb[:, :], in_=self.ps[b][:, :], func=AF.Identity, bias=convb[:, ct:ct + 1]),
                     reads=bk + ["convb"], writes=[sk])
                P.dma(uout[tt * (c.TT // 128):(tt + 1) * (c.TT // 128), :, ct, :].rearrange("q p t -> p q t"),
                      sb[:, :].rearrange("p (q t) -> p q t", t=128), reads=[sk], writes=["uconv"], sem=f"{sk[0]}{sk[1]}", merge=True)

    def attn_phase(self, qT, kT, vtm, oT):
        P, c = self.P, self.c
        XD = c.XD; KH = XD // 128; MT = c.MEM // 128
        scale = float(XD ** -0.5)
        qh = Ring("qh", [P.sbuf(f"qh{i}", [128, KH, c.T], BF16) for i in range(2)])
        pt = Ring("ptb", [P.sbuf(f"ptb{i}", [128, MT, c.TT], BF16) for i in range(2)])
        pf = Ring("pf", [P.sbuf(f"pf{i}", [128, c.MEM], F32) for i in range(2)])
        pb = Ring("pb", [P.sbuf(f"pb{i}", [128, c.MEM], BF16) for i in range(2)])
        sm = Ring("sm", [P.sbuf(f"sm{i}", [128, 4], F32) for i in range(4)])
        qv = qT.rearrange("(k p) t -> p k t", p=128)
        for h in range(c.XH):
            qk, qb = qh.next()
            P.dma(qb[:, :, :], qv[:, h * KH:(h + 1) * KH, :], reads=["qT"], writes=[qk], sem=f"qh{qk[1]}")
            for g in range(c.T // c.TT):
                ptk, ptb = pt.next()
                for i in range(c.TT // 128):
                    t0 = g * c.TT + i * 128
                    hk, hp = self.half()
                    for k in range(KH):
                        P.op("pe", lambda e, k=k: e.matmul(hp[:, 0:c.MEM], lhsT=qb[:, k, t0:t0 + 128], rhs=kT[:, h * KH + k, :],
                                                           start=(k == 0), stop=(k == KH - 1)),
                             reads=[qk, "kT"], writes=[hk], inc=(k == KH - 1), merge=(k > 0))
                    sk, st = sm.next()
                    P.op("dve", lambda e: e.tensor_reduce(out=st[:, 0:1], in_=hp[:, 0:c.MEM], axis=AX.X, op=ALU.max), reads=[hk], writes=[sk])
                    P.op("dve", lambda e: e.tensor_scalar(out=st[:, 1:2], in0=st[:, 0:1], scalar1=-scale, scalar2=None, op0=ALU.mult), reads=[sk], writes=[sk])
                    fk, fb = pf.next()
                    P.op("act", lambda e: e.activation(out=fb[:, :], in_=hp[:, 0:c.MEM], func=AF.Exp, bias=st[:, 1:2], scale=scale, accum_out=st[:, 2:3]),
                         reads=[hk, sk], writes=[fk, sk])
                    P.op("dve", lambda e: e.reciprocal(out=st[:, 3:4], in_=st[:, 2:3]), reads=[sk], writes=[sk])
                    bk_, bb = pb.next()
                    P.op("dve", lambda e: e.tensor_scalar(out=bb[:, :], in0=fb[:, :], scalar1=st[:, 3:4], scalar2=None, op0=ALU.mult), reads=[fk, sk], writes=[bk_])
                    tk, tp = self.half()
                    tpb = tp.bitcast(BF16)
                    for mt in range(MT):
                        P.op("pe", lambda e, mt=mt: e.transpose(tpb[:, mt * 128:(mt + 1) * 128], bb[:, mt * 128:(mt + 1) * 128], self.ident_b),
                             reads=[bk_, "cb"], writes=[tk], inc=(mt == MT - 1), merge=(mt > 0))
                    P.op("act", lambda e: e.activation(out=ptb[:, :, i * 128:(i + 1) * 128],
                                                       in_=tpb[:, 0:MT * 128].rearrange("p (a t) -> p a t", a=MT), func=AF.Copy),
                         reads=[tk], writes=[ptk], merge=(i > 0))
                for dv in range(KH):
                    b = self.bank(); bk = self.bank_keys(b)
                    col = h * XD + dv * 128
                    for mt in range(MT):
                        P.op("pe", lambda e, mt=mt: e.matmul(self.ps[b][:, :], lhsT=vtm[:, mt, col:col + 128], rhs=ptb[:, mt, :],
                                                             start=(mt == 0), stop=(mt == MT - 1)),
                             reads=["vtm", ptk], writes=bk, inc=(mt == MT - 1), merge=(mt > 0))
                    sk2, sb2 = self._st16.next()
                    if dv % 2 == 0:
                        P.op("act", lambda e: e.activation(out=sb2[:, :], in_=self.ps[b][:, :], func=AF.Copy), reads=bk, writes=[sk2])
                    else:
                        P.op("dve", lambda e: e.tensor_copy(out=sb2[:, :], in_=self.ps[b][:, :]), reads=bk, writes=[sk2])
                    P.dma(oT[col:col + 128, g * c.TT:(g + 1) * c.TT], sb2[:, :], reads=[sk2], writes=["oT"], sem=f"{sk2[0]}{sk2[1]}", merge=True)

    def rwkv_phase(self, Z, ZP, orT):
        P, c = self.P, self.c
        SEG = c.SEG; NCH = SEG // 64; NHP = c.NHP; DR = c.DR
        nseg = c.T // SEG
        segs = [("p", s) for s in range(nseg)] + [("m", s) for s in range(nseg)]
        NT = SEG // 512 if SEG >= 512 else 1
        TW = min(512, SEG)
        NV = 10 * NHP + 4
        rvec_in = self.inp("rvec", [128, NV])
        rv = P.sbuf("rv", [128, NV], F32)
        P.dma(rv[:, :], rvec_in[:, :], writes=["rv"], sem="crv")
        MU_R, MU_K, MU_V, W0, A0, KK, KA, RK, GG, GB = range(10)
        pv = lambda i, hp: rv[:, i * NHP + hp:i * NHP + hp + 1]
        omka = P.sbuf("omka", [128, NHP], F32)
        P.op("dve", lambda e: e.tensor_scalar(out=omka[:, :], in0=rv[:, KA * NHP:(KA + 1) * NHP], scalar1=-1.0, scalar2=1.0, op0=ALU.mult, op1=ALU.add),
             reads=["rv"], writes=["omka"])
        W1 = SEG + 1
        zK = P.sbuf("zK", [128, W1], F32); zV = P.sbuf("zV", [128, W1], F32); zR = P.sbuf("zR", [128, W1], F32)
        AI = P.sbuf("AI", [128, SEG], F32)
        t1 = P.sbuf("t1", [128, SEG], F32); t2 = P.sbuf("t2", [128, SEG], F32)
        t3 = P.sbuf("t3", [128, SEG], F32); t4 = P.sbuf("t4", [128, SEG], F32)
        wup_in = self.inp("w_up_t", [128, DR]); aup_in = self.inp("a_up_t", [128, DR]); gup_in = self.inp("g_up_t", [128, 2, DR])
        wup_b = P.sbuf("wup_b", [128, DR], BF16); aup_b = P.sbuf("aup_b", [128, DR], BF16); gup_b = P.sbuf("gup_b", [128, 2, DR], BF16)
        for (src, dst, nm) in [(wup_in[:, :], wup_b[:, :], "wup"), (aup_in[:, :], aup_b[:, :], "aup"),
                               (gup_in[:, 0, :], gup_b[:, 0, :], "gup0"), (gup_in[:, 1, :], gup_b[:, 1, :], "gup1")]:
            for o in range(0, DR, SEG):
                w = min(SEG, DR - o)
                P.dma(t1[:, 0:w], src[:, o:o + w], writes=["t1"], sem="c0")
                P.op("dve", lambda e: e.tensor_copy(out=dst[:, o:o + w], in_=t1[:, 0:w]), reads=["t1"], writes=[nm], merge=True)
        tdw = P.sbuf("tdw", [128, 2 * c.T], BF16); sda = P.sbuf("sda", [128, 2 * c.T], BF16); sdg = P.sbuf("sdg", [128, 2, c.T], BF16)

        def shift(zt, zkey, mu_ap, out_ap, okey, np_=128):
            P.op("pool", lambda e: e.tensor_tensor(out=t1[0:np_, 0:SEG], in0=zt[0:np_, 0:SEG], in1=zt[0:np_, 1:SEG + 1], op=ALU.subtract),
                 reads=[zkey], writes=["t1"])
            P.op("dve", lambda e: e.scalar_tensor_tensor(out=out_ap, in0=t1[0:np_, 0:SEG], scalar=mu_ap, in1=zt[0:np_, 1:SEG + 1],
                                                         op0=ALU.mult, op1=ALU.add), reads=["t1", zkey], writes=[okey])

        for si, (kind, s) in enumerate(segs):
            src = ZP if kind == "p" else Z
            skey = "ZP" if kind == "p" else "Z"
            cb0 = 32 + s * SEG
            off = si * SEG
            mdw = rv[0:96, 10 * NHP:10 * NHP + 1]; mda = rv[0:96, 10 * NHP + 1:10 * NHP + 2]
            P.dma(zK[0:96, :], src[c.C_DW:c.C_DW + 96, cb0 - 1:cb0 + SEG], reads=[skey], writes=["zK"], sem="zK")
            shift(zK, "zK", mdw, zK[0:96, 1:SEG + 1], "zK", 96)
            P.op("act", lambda e: e.activation(out=tdw[0:96, off:off + SEG], in_=zK[0:96, 1:SEG + 1], func=AF.Tanh), reads=["zK"], writes=["tdw"], merge=True)
            P.dma(zV[0:96, :], src[c.C_DA:c.C_DA + 96, cb0 - 1:cb0 + SEG], reads=[skey], writes=["zV"], sem="zV")
            shift(zV, "zV", mda, zV[0:96, 1:SEG + 1], "zV", 96)
            P.op("act", lambda e: e.activation(out=sda[0:96, off:off + SEG], in_=zV[0:96, 1:SEG + 1], func=AF.Copy), reads=["zV"], writes=["sda"], merge=True)
            if kind == "m":
                for k in range(2):
                    mdg = rv[:, 10 * NHP + 2 + k:10 * NHP + 3 + k]
                    P.dma(zR[:, :], src[c.C_DG + k * 128:c.C_DG + (k + 1) * 128, cb0 - 1:cb0 + SEG], reads=[skey], writes=["zR"], sem="zR")
                    shift(zR, "zR", mdg, zR[:, 1:SEG + 1], "zR")
                    P.op("act", lambda e: e.activation(out=sdg[:, k, s * SEG:(s + 1) * SEG], in_=zR[:, 1:SEG + 1], func=AF.Sigmoid),
                         reads=["zR"], writes=["sdg"], merge=True)
        BDAR = [P.sbuf(f"BDAR{i}", [128, NCH, 2, 128], BF16) for i in range(2)]
        BDB = [P.sbuf(f"BDB{i}", [128, NCH, 128], BF16) for i in range(2)]
        BDK = [P.sbuf(f"BDK{i}", [128, NCH, 128], BF16) for i in range(2)]
        BDV = [P.sbuf(f"BDV{i}", [128, NCH, 128], BF16) for i in range(2)]
        E1 = [P.sbuf(f"E1_{i}", [128, SEG], F32) for i in range(3)]
        GB_ = [P.sbuf(f"G_{i}", [128, SEG], F32) for i in range(4)]
        BON = [P.sbuf(f"BON{i}", [128, SEG], F32) for i in range(4)]
        t5 = P.sbuf("t5", [128, SEG], F32)
        OTM = [P.sbuf(f"OTM{i}", [128, NCH, 64], F32) for i in range(2)]
        OBD = P.sbuf("OBD", [128, NCH, 128], BF16)
        gst = P.sbuf("gst", [128, 4, NCH], F32)
        ost = Ring("ost", [P.sbuf(f"ost{i}", [128, 256], BF16) for i in range(2)])
        G = 4
        fctx = []
        for i in range(2):
            d = dict(
                X0T=P.sbuf(f"fX0T{i}", [128, G, 128], BF16), X0=P.sbuf(f"fX0{i}", [128, G, 128], BF16),
                YX=[P.sbuf(f"fYX{i}_{q}", [128, G, 256], BF16) for q in range(2)],
                XT=[P.sbuf(f"fXT{i}_{q}", [128, G, 128], BF16) for q in range(2)],
                MKT=P.sbuf(f"fMKT{i}", [128, G, 128], BF16), ARK=P.sbuf(f"fARK{i}", [128, G, 128], BF16),
                ARB=P.sbuf(f"fARB{i}", [128, G, 128], BF16),
                TM=P.sbuf(f"fTM{i}", [128, G, 448], BF16), TIT=P.sbuf(f"fTIT{i}", [128, G, 128], BF16),
                Q=P.sbuf(f"fQ{i}", [128, G, 64], BF16), BDP=P.sbuf(f"fBDP{i}", [128, G, 128], BF16), i=i)
            P.op("pool", lambda e: e.memset(d["BDP"][:, :, :], 0.0), writes=[("f", i, "BDP")])
            fctx.append(d)
        bctx = [dict(GT=P.sbuf(f"bGT{i}", [128, G, 128], F32), MUN=P.sbuf(f"bMUN{i}", [128, G, 128], F32),
                     H=P.sbuf(f"bH{i}", [128, G, 64], F32), NUN=P.sbuf(f"bNUN{i}", [128, G, 64], F32), i=i) for i in range(4)]
        STS = [[P.sbuf(f"ST{a}_{b}", [128, 64], F32) for b in range(2)] for a in range(2)]
        hm4 = self.hmask.unsqueeze(1).unsqueeze(3).broadcast_to([128, NCH, 2, 64])
        hmn4 = self.hmaskn.unsqueeze(1).unsqueeze(3).broadcast_to([128, NCH, 2, 64])

        def bd(eng, x_ap, out4, mask4, rkeys, wkey):
            x4 = x_ap.rearrange("p (c t) -> p c t", t=64).unsqueeze(2).broadcast_to([128, NCH, 2, 64])
            P.op(eng, lambda e: e.tensor_tensor(out=out4, in0=x4, in1=mask4, op=ALU.mult), reads=list(rkeys) + ["cf"], writes=[wkey])

        def v4(t3d):
            return t3d.rearrange("p c (h t) -> p c h t", h=2)

        def lora_tile(wb, wkey, rhs, rkey, hp, off, tt, nk):
            b = self.bank(); bk = self.bank_keys(b)
            for k in range(nk):
                if nk == 1:
                    l = wb[0:96, hp * 128:(hp + 1) * 128]; r = rhs[0:96, off + tt * TW:off + (tt + 1) * TW]
                else:
                    l = wb[:, k, hp * 128:(hp + 1) * 128]; r = rhs[:, k, off + tt * TW:off + (tt + 1) * TW]
                P.op("pe", lambda e, k=k: e.matmul(self.ps[b][:, 0:TW], lhsT=l, rhs=r, start=(k == 0), stop=(k == nk - 1)),
                     reads=[wkey, rkey], writes=bk, inc=(k == nk - 1), merge=(k > 0))
            return bk, self.ps[b][:, 0:TW]

        def headsum_tile(x, xkey, tt):
            b = self.bank(); bk = self.bank_keys(b)
            P.op("pe", lambda e: e.matmul(self.ps[b][:, 0:TW], lhsT=self.bones_f, rhs=x[:, tt * TW:(tt + 1) * TW], start=True, stop=True),
                 reads=["cf", xkey], writes=bk)
            return bk, self.ps[b][:, 0:TW]

        stq = [0, 0]
        bci = [0]
        pend_chain = []
        pend_post = []
        m3 = lambda m: m.unsqueeze(1).broadcast_to([128, G, 128])
        g3 = lambda ap, w: ap.rearrange("p (g t) -> p g t", g=G)[:, :, 0:w] if w else ap.rearrange("p (g t) -> p g t", g=G)

        def fk(grp, f):
            return ("f", grp["x"]["i"], f)

        def bkk(grp, f):
            return ("b", grp["y"]["i"], f)

        def rd(grp):
            d = grp["itd"]; q = d["q2"]
            return [("BDA", q), ("BDB", q), ("BDK", q), ("BDV", q)] + ([("BDR", q)] if d["main"] else [])

        def gmm(grp, w, lhs_fn, rhs_fn, reads, nacc=1):
            b = self.bank(); bk = ("psb", b)
            for g in range(G):
                for a in range(nacc):
                    P.op("pe", lambda e, g=g, a=a: e.matmul(self.ps[b][:, g * w:(g + 1) * w], lhsT=lhs_fn(g, a), rhs=rhs_fn(g, a),
                                                          start=(a == 0), stop=(a == nacc - 1)),
                         reads=reads, writes=[bk], inc=(g == G - 1 and a == nacc - 1), merge=True)
            return bk, self.ps[b][:, 0:G * w].rearrange("p (g t) -> p g t", g=G)

        def stage1(grp):
            d = grp["itd"]; x = grp["x"]; c0_ = grp["ch0"]; main_ = d["main"]
            bdar_, bdb_, bdk_, bdv_ = d["bdar"], d["bdb"], d["bdk"], d["bdv"]
            R_ = rd(grp)
            kA, pA = gmm(grp, 128, lambda g, a: bdar_[:, c0_ + g, 0, :], lambda g, a: bdb_[:, c0_ + g, :], R_)
            P.op("dve", lambda e: e.tensor_tensor(out=x["X0T"][:, :, :], in0=pA, in1=m3(self.mask_sl), op=ALU.mult), reads=[kA, "cf"], writes=[fk(grp, "X0T")])
            kB, pB = gmm(grp, 128, lambda g, a: bdb_[:, c0_ + g, :], lambda g, a: bdar_[:, c0_ + g, 0, :], R_)
            P.op("dve", lambda e: e.tensor_tensor(out=x["X0"][:, :, :], in0=pB, in1=m3(self.mask_su), op=ALU.mult), reads=[kB, "cf"], writes=[fk(grp, "X0")])
            P.op("pool", lambda e: e.tensor_tensor(out=x["YX"][1][:, :, 0:128], in0=x["X0"][:, :, :], in1=m3(self.ident_b), op=ALU.add),
                 reads=[fk(grp, "X0"), "cb"], writes=[fk(grp, "YXy1")])
            kC, pC = gmm(grp, 128, lambda g, a: bdk_[:, c0_ + g, :], lambda g, a: bdar_[:, c0_ + g, 0, :], R_)
            P.op("dve", lambda e: e.tensor_tensor(out=x["MKT"][:, :, :], in0=pC, in1=m3(self.mask_su), op=ALU.mult), reads=[kC, "cf"], writes=[fk(grp, "MKT")])
            if main_:
                mu_ = self.mask_su_u[:, 128:256]
                kB2, pB2 = gmm(grp, 128, lambda g, a: bdb_[:, c0_ + g, :], lambda g, a: bdar_[:, c0_ + g, 1, :], R_)
                P.op("dve", lambda e: e.tensor_tensor(out=x["ARB"][:, :, :], in0=pB2, in1=m3(mu_), op=ALU.mult), reads=[kB2, "cf"], writes=[fk(grp, "ARB")])
                kC2, pC2 = gmm(grp, 128, lambda g, a: bdk_[:, c0_ + g, :], lambda g, a: bdar_[:, c0_ + g, 1, :], R_)
                P.op("dve", lambda e: e.tensor_tensor(out=x["ARK"][:, :, :], in0=pC2, in1=m3(mu_), op=ALU.mult), reads=[kC2, "cf"], writes=[fk(grp, "ARK")])
            b = self.bank(); kD = ("psb", b)
            for g in range(G):
                P.op("pe", lambda e, g=g: e.matmul(self.ps[b][:, g * 128:g * 128 + 64], lhsT=bdar_[:, c0_ + g, 0, :], rhs=self.isel_b, start=True, stop=True),
                     reads=R_ + ["cb"], writes=[kD], inc=False, merge=True)
                P.op("pe", lambda e, g=g: e.matmul(self.ps[b][:, g * 128 + 64:g * 128 + 128], lhsT=bdv_[:, c0_ + g, :], rhs=self.isel_b, start=True, stop=True),
                     reads=R_ + ["cb"], writes=[kD], inc=(g == G - 1), merge=True)
            pDv = self.ps[b][:, :].rearrange("p (g a t) -> p g a t", g=G, a=2)
            tmv = x["TM"][:, :, 0:256].rearrange("p g (a t) -> p g a t", a=2)[:, :, :, 0:64]
            P.op("act", lambda e: e.activation(out=tmv, in_=pDv, func=AF.Copy), reads=[kD], writes=[fk(grp, "TMa")])
            kE, pE = gmm(grp, 128, lambda g, a: bdb_[:, c0_ + g, :], lambda g, a: self.ident_b, R_ + ["cb"])
            P.op("act", lambda e: e.activation(out=x["TM"][:, :, 192:320], in_=pE, func=AF.Copy), reads=[kE], writes=[fk(grp, "TMb")])
            kE2, pE2 = gmm(grp, 128, lambda g, a: bdk_[:, c0_ + g, :], lambda g, a: self.ident_b, R_ + ["cb"])
            P.op("act", lambda e: e.activation(out=x["TM"][:, :, 320:448], in_=pE2, func=AF.Copy), reads=[kE2], writes=[fk(grp, "TMk")])

        def stage2(grp):
            x = grp["x"]
            kF, pF = gmm(grp, 128, lambda g, a: x["X0T"][:, g, :], lambda g, a: x["X0"][:, g, :], [fk(grp, "X0T"), fk(grp, "X0")])
            P.op("act", lambda e: e.activation(out=x["YX"][1][:, :, 128:256], in_=pF, func=AF.Copy), reads=[kF], writes=[fk(grp, "YXx1")])
            kG, pG = gmm(grp, 128, lambda g, a: x["X0"][:, g, :], lambda g, a: x["X0T"][:, g, :], [fk(grp, "X0T"), fk(grp, "X0")])
            P.op("act", lambda e: e.activation(out=x["XT"][1][:, :, :], in_=pG, func=AF.Copy), reads=[kG], writes=[fk(grp, "XTq1")])
            kK, pK = gmm(grp, 64, lambda g, a: x["MKT"][:, g, :], lambda g, a: x["TM"][:, g, 128:192], [fk(grp, "MKT"), fk(grp, "TMa")])
            P.op("act", lambda e: e.activation(out=x["TM"][:, :, 64:128], in_=pK, func=AF.Copy), reads=[kK], writes=[fk(grp, "MV")])

        def dstep(grp, q):
            x = grp["x"]; n = 1 - q
            kH, pH = gmm(grp, 128, lambda g, a: x["XT"][q][:, g, :], lambda g, a: x["YX"][q][:, g, 0:128], [fk(grp, f"XTq{q}"), fk(grp, f"YXy{q}")])
            P.op("dve", lambda e: e.tensor_tensor(out=x["YX"][n][:, :, 0:128], in0=pH, in1=x["YX"][q][:, :, 0:128], op=ALU.add),
                 reads=[kH, fk(grp, f"YXy{q}")], writes=[fk(grp, f"YXy{n}")])
            kH2, pH2 = gmm(grp, 128, lambda g, a: x["XT"][q][:, g, :], lambda g, a: x["YX"][q][:, g, 128:256], [fk(grp, f"XTq{q}"), fk(grp, f"YXx{q}")])
            P.op("act", lambda e: e.activation(out=x["YX"][n][:, :, 128:256], in_=pH2, func=AF.Copy), reads=[kH2], writes=[fk(grp, f"YXx{n}")])
            kI, pI = gmm(grp, 128, lambda g, a: x["YX"][q][:, g, 128:256], lambda g, a: x["XT"][q][:, g, :], [fk(grp, f"XTq{q}"), fk(grp, f"YXx{q}")])
            P.op("act", lambda e: e.activation(out=x["XT"][n][:, :, :], in_=pI, func=AF.Copy), reads=[kI], writes=[fk(grp, f"XTq{n}")])

        def stage7(grp):
            x = grp["x"]
            kJ, pJ = gmm(grp, 128, lambda g, a: x["XT"][1][:, g, :], lambda g, a: x["YX"][1][:, g, 0:128], [fk(grp, "XTq1"), fk(grp, "YXy1")])
            P.op("dve", lambda e: e.tensor_tensor(out=x["TIT"][:, :, :], in0=pJ, in1=x["YX"][1][:, :, 0:128], op=ALU.add),
                 reads=[kJ, fk(grp, "YXy1")], writes=[fk(grp, "TIT")])

        def stage8(grp):
            x = grp["x"]
            kL, pL = gmm(grp, 128, lambda g, a: x["TIT"][:, g, :], lambda g, a: x["TM"][:, g, 0:128], [fk(grp, "TIT"), fk(grp, "TMa"), fk(grp, "MV")])
            P.op("act", lambda e: e.activation(out=x["Q"][:, :, :], in_=pL[:, :, 64:128], func=AF.Copy), reads=[kL], writes=[fk(grp, "Q")])
            P.op("act", lambda e: e.activation(out=x["BDP"][0:64, :, 0:64], in_=pL[0:64, :, 0:64], func=AF.Copy), reads=[kL], writes=[fk(grp, "BDP")])
            P.op("act", lambda e: e.activation(out=x["BDP"][64:128, :, 64:128], in_=pL[64:128, :, 0:64], func=AF.Copy), reads=[kL], writes=[fk(grp, "BDP")], merge=True)

        def stage9(grp):
            d = grp["itd"]; x = grp["x"]; y = grp["y"]; c0_ = grp["ch0"]; main_ = d["main"]
            e1_ = d["e1"]; q = d["q2"]
            wcb = e1_[:, c0_ * 64:(c0_ + G) * 64].rearrange("p (g t) -> p g t", t=64)[:, :, 63:64].broadcast_to([128, G, 64])
            if main_:
                kM, pM = gmm(grp, 128, lambda g, a: x["BDP"][:, g, :], lambda g, a: x["ARB"][:, g, :], [fk(grp, "BDP"), fk(grp, "ARB")])
                P.op("dve", lambda e: e.tensor_tensor(out=y["GT"][:, :, :], in0=pM, in1=d["bdar"][:, c0_:c0_ + G, 1, :], op=ALU.add),
                     reads=[kM, ("BDR", q)], writes=[bkk(grp, "GT")])
            kN, pN = gmm(grp, 128, lambda g, a: x["BDP"][:, g, :], lambda g, a: x["TM"][:, g, 192:320], [fk(grp, "BDP"), fk(grp, "TMb")])
            P.op("dve", lambda e: e.tensor_tensor(out=y["MUN"][:, :, :], in0=pN, in1=m3(self.ident_f), op=ALU.add), reads=[kN, "cf"], writes=[bkk(grp, "MUN")])
            if main_:
                kO, pO = gmm(grp, 64, lambda g, a: (x["ARB"][:, g, :] if a == 0 else x["ARK"][:, g, :]),
                             lambda g, a: (x["Q"][:, g, :] if a == 0 else x["TM"][:, g, 128:192]),
                             [fk(grp, "ARB"), fk(grp, "ARK"), fk(grp, "Q"), fk(grp, "TMa")], nacc=2)
                P.op("act", lambda e: e.activation(out=y["H"][:, :, :], in_=pO, func=AF.Copy), reads=[kO], writes=[bkk(grp, "H")])
            kP, pP = gmm(grp, 64, lambda g, a: (x["TM"][:, g, 192:320] if a == 0 else x["TM"][:, g, 320:448]),
                         lambda g, a: (x["Q"][:, g, :] if a == 0 else x["TM"][:, g, 128:192]),
                         [fk(grp, "TMb"), fk(grp, "TMk"), fk(grp, "Q"), fk(grp, "TMa")], nacc=2)
            P.op("dve", lambda e: e.tensor_tensor(out=y["NUN"][:, :, :], in0=pP, in1=wcb, op=ALU.mult), reads=[kP, ("E1", d["e3"])], writes=[bkk(grp, "NUN")])

        def chain_unit(grp, g):
            d = grp["itd"]; y = grp["y"]; ch = grp["ch0"] + g; q = d["q2"]; hpp = d["hp"] % 2
            wc = d["e1"][:, ch * 64 + 63:ch * 64 + 64]
            sq = stq[hpp]
            so, sn = STS[hpp][sq], STS[hpp][1 - sq]
            if d["main"]:
                b = self.bank(); kO2 = ("psb", b); pO2 = self.ps[b][:, 0:64]
                P.op("pe", lambda e: e.matmul(pO2, lhsT=y["GT"][:, g, :], rhs=so[:, :], start=True, stop=True),
                     reads=[bkk(grp, "GT"), ("ST", hpp, sq)], writes=[kO2])
                P.op("dve", lambda e: e.tensor_tensor(out=d["otm"][:, ch, :], in0=pO2, in1=y["H"][:, g, :], op=ALU.add),
                     reads=[kO2, bkk(grp, "H")], writes=[("OTM", q)], merge=True)
            b = self.bank(); kS = ("psb", b); pS = self.ps[b][:, 0:64]
            P.op("pe", lambda e: e.matmul(pS, lhsT=y["MUN"][:, g, :], rhs=so[:, :], start=True, stop=True),
                 reads=[bkk(grp, "MUN"), ("ST", hpp, sq)], writes=[kS])
            P.op("dve", lambda e: e.scalar_tensor_tensor(out=sn[:, :], in0=pS, scalar=wc, in1=y["NUN"][:, g, :], op0=ALU.mult, op1=ALU.add),
                 reads=[kS, bkk(grp, "NUN"), ("E1", d["e3"])], writes=[("ST", hpp, 1 - sq)])
            stq[hpp] = 1 - sq

        def post_gen(d):
            hp_, s_, q = d["hp"], d["s"], d["q2"]
            otm, gbuf, bon = d["otm"], d["gbuf"], d["bon"]
            kO_ = ("OTM", q); kB_ = ("BON", d["g4"]); kG_ = ("G", d["g4"])
            mu, var, rstd = gst[:, 0, :], gst[:, 1, :], gst[:, 2, :]
            t5v = t5[:, :].rearrange("p (c t) -> p c t", t=64)
            P.op("dve", lambda e: e.tensor_reduce(out=mu, in_=otm[:, :, :], axis=AX.X, op=ALU.add), reads=[kO_], writes=["gst0"])
            yield
            P.op("dve", lambda e: e.tensor_scalar(out=mu, in0=mu, scalar1=1.0 / 64, scalar2=None, op0=ALU.mult), reads=["gst0"], writes=["gst0"])
            yield
            P.op("pool", lambda e: e.tensor_tensor(out=otm[:, :, :], in0=otm[:, :, :], in1=mu.unsqueeze(2).broadcast_to([128, NCH, 64]), op=ALU.subtract),
                 reads=[kO_, "gst0"], writes=[kO_])
            yield
            P.op("act", lambda e: e.activation(out=t5v, in_=otm[:, :, :], func=AF.Square), reads=[kO_], writes=["t5"])
            yield
            P.op("dve", lambda e: e.tensor_reduce(out=var, in_=t5v, axis=AX.X, op=ALU.add), reads=["t5"], writes=["gst1"])
            yield
            P.op("dve", lambda e: e.tensor_scalar(out=var, in0=var, scalar1=1.0 / 64, scalar2=64e-5, op0=ALU.mult, op1=ALU.add), reads=["gst1"], writes=["gst1"])
            yield
            P.op("act", lambda e: e.activation(out=var, in_=var, func=AF.Sqrt), reads=["gst1"], writes=["gst1"])
            yield
            P.op("dve", lambda e: e.reciprocal(out=rstd, in_=var), reads=["gst1"], writes=["gst2"])
            yield
            P.op("dve", lambda e: e.tensor_tensor(out=otm[:, :, :], in0=otm[:, :, :], in1=rstd.unsqueeze(2).broadcast_to([128, NCH, 64]), op=ALU.mult),
                 reads=[kO_, "gst2"], writes=[kO_])
            yield
            o4 = otm[:, :, :].unsqueeze(2).broadcast_to([128, NCH, 2, 64])
            P.op("pool", lambda e: e.tensor_tensor(out=v4(OBD[:, :, :]), in0=o4, in1=hm4, op=ALU.mult), reads=[kO_, "cf"], writes=["OBD"])
            yield
            for c4 in range(0, NCH, 4):
                b = self.bank(); kT = ("psb", b); pT = self.ps[b][:, 0:256]
                for q_ in range(4):
                    P.op("pe", lambda e, q_=q_: e.matmul(pT[:, q_ * 64:(q_ + 1) * 64], lhsT=OBD[:, c4 + q_, :], rhs=self.isel_b, start=True, stop=True),
                         reads=["OBD", "cb"], writes=[kT], inc=(q_ == 3), merge=True)
                cs0 = c4 * 64
                P.op("dve", lambda e: e.tensor_scalar(out=t5[:, cs0:cs0 + 256], in0=pT, scalar1=pv(GG, hp_), scalar2=pv(GB, hp_), op0=ALU.mult, op1=ALU.add),
                     reads=[kT, "rv"], writes=["t5"], merge=True)
                yield
                P.op("pool", lambda e: e.tensor_tensor(out=t5[:, cs0:cs0 + 256], in0=t5[:, cs0:cs0 + 256], in1=bon[:, cs0:cs0 + 256], op=ALU.add),
                     reads=["t5", kB_], writes=["t5"], merge=True)
                yield
                ok_, ob = ost.next()
                P.op("pool", lambda e: e.tensor_tensor(out=ob[:, :], in0=t5[:, cs0:cs0 + 256], in1=gbuf[:, cs0:cs0 + 256], op=ALU.mult),
                     reads=["t5", kG_], writes=[ok_])
                P.dma(orT[hp_ * 128:(hp_ + 1) * 128, s_ * SEG + cs0:s_ * SEG + cs0 + 256], ob[:, :], reads=[ok_], writes=["orT"], sem=f"ost{ok_[1]}", merge=True)
                yield

        def prep_gen(d):
            hp, si, kind, s, main, q2 = d["hp"], d["si"], d["kind"], d["s"], d["main"], d["q2"]
            src = ZP if kind == "p" else Z
            skey = "ZP" if kind == "p" else "Z"
            cb0 = 32 + s * SEG
            off = si * SEG
            bdar, bdb, bdk, bdv, e1 = d["bdar"], d["bdb"], d["bdk"], d["bdv"], d["e1"]
            e3i, g4i = d["e3"], d["g4"]
            kq = lambda n: (n, q2)
            if si == 0:
                P.op("pool", lambda e: e.memset(STS[hp % 2][0][:, :], 0.0), writes=[("ST", hp % 2, 0)])
                stq[hp % 2] = 0
            P.dma(zK[:, :], src[c.C_K + hp * 128:c.C_K + (hp + 1) * 128, cb0 - 1:cb0 + SEG], reads=[skey], writes=["zK"], sem="zK")
            yield
            shift(zK, "zK", pv(MU_K, hp), zK[:, 1:W1], "zK")
            yield
            K = zK[:, 1:W1]
            P.dma(zV[:, :], src[c.C_V + hp * 128:c.C_V + (hp + 1) * 128, cb0 - 1:cb0 + SEG], reads=[skey], writes=["zV"], sem="zV")
            yield
            shift(zV, "zV", pv(MU_V, hp), zV[:, 1:W1], "zV")
            yield
            V = zV[:, 1:W1]
            if main:
                P.dma(zR[:, :], src[hp * 128:(hp + 1) * 128, cb0 - 1:cb0 + SEG], reads=[skey], writes=["zR"], sem="zR")
                shift(zR, "zR", pv(MU_R, hp), zR[:, 1:W1], "zR")
            yield
            R = zR[:, 1:W1]
            for tt in range(NT):
                bk, ps = lora_tile(aup_b, "aup", sda, "sda", hp, off, tt, 1)
                P.op("act", lambda e: e.activation(out=AI[:, tt * TW:(tt + 1) * TW], in_=ps, func=AF.Sigmoid, bias=pv(A0, hp)),
                     reads=bk + ["rv"], writes=["AI"], merge=(tt > 0))
            yield
            P.op("pool", lambda e: e.tensor_scalar(out=t2[:, :], in0=K, scalar1=pv(KK, hp), scalar2=None, op0=ALU.mult), reads=["zK", "rv"], writes=["t2"])
            yield
            P.op("act", lambda e: e.activation(out=t3[:, :], in_=t2[:, :], func=AF.Square), reads=["t2"], writes=["t3"])
            yield
            for tt in range(NT):
                bk, ps = headsum_tile(t3, "t3", tt)
                P.op("act", lambda e: e.activation(out=t4[:, tt * TW:(tt + 1) * TW], in_=ps, func=AF.Sqrt), reads=bk, writes=["t4"], merge=(tt > 0))
            yield
            P.op("dve", lambda e: e.tensor_scalar(out=t4[:, :], in0=t4[:, :], scalar1=1e-12, scalar2=None, op0=ALU.max), reads=["t4"], writes=["t4"])
            yield
            P.op("dve", lambda e: e.reciprocal(out=t4[:, :], in_=t4[:, :]), reads=["t4"], writes=["t4"])
            yield
            P.op("pool", lambda e: e.tensor_tensor(out=t2[:, :], in0=t2[:, :], in1=t4[:, :], op=ALU.mult), reads=["t2", "t4"], writes=["t2"])
            yield
            P.op("dve", lambda e: e.tensor_scalar(out=t3[:, :], in0=AI[:, :], scalar1=pv(KA, hp), scalar2=omka[:, hp:hp + 1], op0=ALU.mult, op1=ALU.add),
                 reads=["AI", "rv", "omka"], writes=["t3"])
            yield
            P.op("pool", lambda e: e.tensor_tensor(out=K, in0=K, in1=t3[:, :], op=ALU.mult), reads=["zK", "t3"], writes=["zK"])
            yield
            P.op("dve", lambda e: e.tensor_tensor(out=AI[:, :], in0=AI[:, :], in1=t2[:, :], op=ALU.mult), reads=["AI", "t2"], writes=["AI"])
            yield
            for tt in range(NT):
                bk, ps = lora_tile(wup_b, "wup", tdw, "tdw", hp, off, tt, 1)
                P.op("act", lambda e: e.activation(out=t3[:, tt * TW:(tt + 1) * TW], in_=ps, func=AF.Sigmoid, bias=pv(W0, hp)),
                     reads=bk + ["rv"], writes=["t3"], merge=(tt > 0))
            yield
            P.op("dve", lambda e: e.tensor_tensor_scan(out=t4[:, :], data0=self.reset[:, 0:SEG], data1=t3[:, :], initial=0.0, op0=ALU.mult, op1=ALU.add),
                 reads=["t3", "cf"], writes=["t4"])
            yield
            if main:
                bon = d["bon"]
                P.op("dve", lambda e: e.scalar_tensor_tensor(out=t1[:, :], in0=R, scalar=pv(RK, hp), in1=K, op0=ALU.mult, op1=ALU.mult),
                     reads=["zR", "zK", "rv"], writes=["t1"])
                for tt in range(NT):
                    bk, ps = headsum_tile(t1, "t1", tt)
                    P.op("dve", lambda e: e.tensor_tensor(out=bon[:, tt * TW:(tt + 1) * TW], in0=ps, in1=V[:, tt * TW:(tt + 1) * TW], op=ALU.mult),
                         reads=bk + ["zV"], writes=[("BON", g4i)], merge=(tt > 0))
            yield
            yield
            P.op("act", lambda e: e.activation(out=t1[:, :], in_=t4[:, :], func=AF.Exp, scale=DECAY_C), reads=["t4"], writes=["t1"])
            yield
            P.op("dve", lambda e: e.tensor_tensor(out=K, in0=K, in1=t1[:, :], op=ALU.mult), reads=["zK", "t1"], writes=["zK"])
            yield
            P.op("pool", lambda e: e.tensor_tensor(out=AI[:, :], in0=AI[:, :], in1=t1[:, :], op=ALU.mult), reads=["AI", "t1"], writes=["AI"])
            yield
            bd("dve", K, v4(bdk[:, :, :]), hm4, ["zK"], kq("BDK"))
            yield
            bd("pool", AI[:, :], v4(bdb[:, :, :]), hm4, ["AI"], kq("BDB"))
            yield
            yield
            P.op("pool", lambda e: e.tensor_tensor(out=t1[:, :], in0=t4[:, :], in1=t3[:, :], op=ALU.subtract), reads=["t4", "t3"], writes=["t1"])
            yield
            P.op("act", lambda e: e.activation(out=t1[:, :], in_=t1[:, :], func=AF.Exp, scale=-DECAY_C), reads=["t1"], writes=["t1"])
            yield
            P.op("dve", lambda e: e.tensor_tensor(out=t2[:, :], in0=t2[:, :], in1=t1[:, :], op=ALU.mult), reads=["t2", "t1"], writes=["t2"])
            yield
            bd("pool", t2[:, :], v4(bdar[:, :, 0, :]), hmn4, ["t2"], kq("BDA"))
            yield
            yield
            P.op("act", lambda e: e.activation(out=e1[:, :], in_=t4[:, :], func=AF.Exp, scale=-DECAY_C), reads=["t4"], writes=[("E1", e3i)])
            yield
            bd("dve", V, v4(bdv[:, :, :]), hm4, ["zV"], kq("BDV"))
            yield
            if main:
                P.op("dve", lambda e: e.tensor_tensor(out=R, in0=R, in1=e1[:, :], op=ALU.mult), reads=["zR", ("E1", e3i)], writes=["zR"])
                bd("pool", R, v4(bdar[:, :, 1, :]), hm4, ["zR"], kq("BDR"))
                gbuf = d["gbuf"]
                for tt in range(NT):
                    bk, ps = lora_tile(gup_b, "gup0", sdg, "sdg", hp, s * SEG, tt, 2)
                    P.op("act", lambda e: e.activation(out=gbuf[:, tt * TW:(tt + 1) * TW], in_=ps, func=AF.Copy), reads=bk + ["gup1"], writes=[("G", g4i)], merge=(tt > 0))
            yield

        iters = [(hp, si) for hp in range(NHP) for si in range(len(segs))]
        infos = {}

        def start_prep(j):
            hp, si = iters[j]
            kind, s_ = segs[si]
            q2 = j % 2
            d = dict(hp=hp, si=si, kind=kind, s=s_, main=(kind == "m"), q2=q2, e3=j % 3, g4=j % 4,
                     bdar=BDAR[q2], bdb=BDB[q2], bdk=BDK[q2], bdv=BDV[q2], e1=E1[j % 3], otm=OTM[q2],
                     gbuf=GB_[j % 4], bon=BON[j % 4], groups=[])
            infos[j] = d
            return prep_gen(d)

        def advance(gen, n):
            if gen is None:
                return None
            for _ in range(n):
                try:
                    next(gen)
                except StopIteration:
                    return None
            return gen

        advance(start_prep(0), 10 ** 6)
        for j in range(len(iters)):
            itd = infos[j]
            nxt = start_prep(j + 1) if j + 1 < len(iters) else None
            pst = post_gen(infos[j - 2]) if (j >= 2 and infos[j - 2]["main"]) else None
            for gi, ch0 in enumerate(range(0, NCH, G)):
                grp = dict(itd=itd, x=fctx[gi % 2], y=bctx[bci[0] % 4], ch0=ch0)
                bci[0] += 1
                itd["groups"].append(grp)
            grps = itd["groups"]
            sched = [stage1, stage2, lambda g_: dstep(g_, 1), lambda g_: dstep(g_, 0), lambda g_: dstep(g_, 1),
                     lambda g_: dstep(g_, 0), stage7, stage8, stage9]
            for st_fn in sched:
                for grp in grps:
                    st_fn(grp)
                for _ in range(2):
                    if pend_chain:
                        chain_unit(*pend_chain.pop(0))
                nxt = advance(nxt, 7)
                pst = advance(pst, 4)
            while pend_chain:
                chain_unit(*pend_chain.pop(0))
            advance(nxt, 10 ** 6)
            advance(pst, 10 ** 6)
            for grp in grps:
                for g in range(G):
                    pend_chain.append((grp, g))
        while pend_chain:
            chain_unit(*pend_chain.pop(0))
        for j in (len(iters) - 2, len(iters) - 1):
            if j >= 0 and infos[j]["main"]:
                advance(post_gen(infos[j]), 10 ** 6)


def build(cfg, debug=False):
    B = _build(cfg, debug)
    B.P.es.close()
    return B


def _build(cfg, debug=False):
    c = cfg
    B = Builder(cfg, debug)
    P = B.P
    KT, T, TH, D = c.KT, c.T, c.TH, c.D
    x_main = B.inp("x_main", [T, D]); x_prev = B.inp("x_prev", [T, D]); mem = B.inp("mem_b", [c.MEM, D])
    w_in_t = B.inp("w_in_t", [c.NJIN, 128, KT, 128])
    proj_r_t = B.inp("proj_r_t", [KT, 128, c.DR // 128, 128]); proj_c_t = B.inp("proj_c_t", [KT, 128, c.DC // 128, 128])
    w_out_t = B.inp("w_out_t", [KT, 128, KT, 128]); wq_t = B.inp("wq_t", [KT, 128, KT, 128]); wk_t = B.inp("wk_t", [KT, 128, KT, 128])
    wv_t = B.inp("wv_t", [KT, 128, KT, 128]); wo_t = B.inp("wo_t", [KT, 128, KT, 128])
    w1_t = B.inp("w1_t", [c.DFF // 128, 128, KT, 128]); w2_t = B.inp("w2_t", [c.DFF // D, KT, 128, KT, 128])
    lnv_in = B.inp("lnv", [128, 4, 2, KT]); NCT = c.DC // 128
    convv_in = B.inp("convv", [128, NCT, 32]); clnv_in = B.inp("clnv", [128, 2, NCT])
    out = B.nc.dram_tensor("out", [T, D], F32, kind="ExternalOutput").ap()
    dr = P.dram
    Z = dr("Z", [c.NJIN * 128, TH], F32); ZP = dr("ZP", [c.NJIN * 128, TH], F32)
    tl = lambda nm, ktn, nt: dr(nm, [nt // 128, 128, ktn, 128], F32)
    tv = lambda Yt: (lambda j, m, c0, n: Yt[c0 // 128:(c0 + n) // 128, 0:m, j, :].rearrange("q p t -> p q t"))
    xTf = tl("xTf", KT, T); memT = tl("memT", KT, c.MEM)
    orT = dr("orT", [c.DR, T], BF16); uconv = tl("uconv", c.DC // 128, T); ocT = dr("ocT", [c.DC, T], BF16)
    mr = dr("mr", [D, T], F32); mg = dr("mg", [D, T], BF16); y1 = tl("y1", KT, T); h1 = tl("h1", KT, T)
    qT = dr("qT", [D, T], BF16); oT = dr("oT", [D, T], BF16); y2 = tl("y2", KT, T); h2 = tl("h2", KT, T)
    aT = dr("aT", [c.DFF, T], BF16); y3 = [dr(f"y3_{i}", [D, T], F32) for i in range(c.DFF // D - 1)] + [tl(f"y3_{c.DFF // D - 1}", KT, T)]

    B.setup_consts()
    lnv = P.sbuf("lnv", [128, 4, 2, KT], F32)
    P.dma(lnv[:, :, :, :], lnv_in[:, :, :, :], writes=["gb_ln1", "gb_ln2", "gb_ln3", "gb_lnm"], sem="clnv")

    main_tiles = [(i * c.TT, c.TT) for i in range(T // c.TT)]
    halo_tiles = [(0, 32)] + [(32 + i * c.TT, c.TT) for i in range(T // c.TT)]
    full = lambda j: 128

    with P.scope():
        xt = P.sbuf("xt", [128, KT, TH], BF16)
        for ps_ in ("p", "m"):
            with P.scope():
                B.alloc_ti()
                if ps_ == "p":
                    P.op("pool", lambda e: e.memset(xt[:, :, 0:32], 0.0), writes=["xt"], merge=True)
                    B.transpose_in(x_prev, T, xt, "xt", 32)
                else:
                    B.transpose_in(x_prev[T - 32:T, :], 32, xt, "xt", 0)
                    B.transpose_in(x_main, T, xt, "xt", 32, f32_dst=xTf, tag="xTf")
                    B.transpose_in(mem, c.MEM, None, None, 0, f32_dst=memT, tag="memT")
                    if debug == "p0":
                        P.finish(["xTf_f32", "memT_f32"]); return B
            with P.scope():
                B.alloc_mm()
                if ps_ == "p":
                    jl = list(range(c.C_K // 128, (c.C_DA + 96 + 127) // 128))
                    epi = B.std_epilogue(lambda j, m, c0, n: ZP[j * 128:j * 128 + m, c0:c0 + n], F32, "ZP")
                else:
                    jl = list(range(c.NJIN))
                    epi = B.std_epilogue(lambda j, m, c0, n: Z[j * 128:j * 128 + m, c0:c0 + n], F32, "Z")
                B.mm_phase(xt, ["xt"], KT, halo_tiles, w_in_t, jl, lambda j: min(128, c.DIN - j * 128), epi)
    if debug == "p1":
        P.finish(["Z", "ZP", "xTf_f32", "memT_f32"]); return B
    with P.scope():
        B.rwkv_phase(Z, ZP, orT)
    if debug == "p2":
        P.finish(["orT"]); return B
    with P.scope():
        convv = P.sbuf("convv", [128, NCT, 32], F32)
        P.dma(convv[:, :, :], convv_in[:, :, :], writes=["convw", "convb"], sem="ccv")
        convw = convv[:, :, 0:31]; convb = convv[:, :, 31]
        B._st32 = B.stager("st32c_", 2, [128, 512], F32)
        B.conv_phase(Z, convw, convb, uconv)
    with P.scope():
        clnv = P.sbuf("clnv", [128, 2, NCT], F32)
        P.dma(clnv[:, :, :], clnv_in[:, :, :], writes=["gb_cln"], sem="ccl")
        B.alloc_ln(with_out_bf=True)
        B.ln_phase(uconv, "uconv", NCT, T, clnv, 1, 1e-5, out_bf=ocT, out_bf_key="ocT", silu=True, name="cln")
    if debug == "p3":
        P.finish(["ocT"]); return B
    G0 = c.C_GATE
    with P.scope():
        xt2 = P.sbuf("xt2", [128, KT, T], BF16)
        with P.scope():
            B.alloc_mm()
            B.load_xt(xt2, "xt2", orT, "orT", c.DR // 128, T)
            epi = B.std_epilogue(lambda j, m, c0, n: mr[j * 128:j * 128 + m, c0:c0 + n], F32, "mr",
                                 gate_fn=lambda j, m, c0, n: Z[G0 + j * 128:G0 + j * 128 + m, 32 + c0:32 + c0 + n], gate_key="Z")
            B.mm_phase(xt2, ["xt2"], c.DR // 128, main_tiles, proj_r_t, list(range(KT)), full, epi)
            B.load_xt(xt2, "xt2", ocT, "ocT", c.DC // 128, T)
            epi = B.std_epilogue(lambda j, m, c0, n: mg[j * 128:j * 128 + m, c0:c0 + n], BF16, "mg",
                                 gate_fn=lambda j, m, c0, n: Z[G0 + D + j * 128:G0 + D + j * 128 + m, 32 + c0:32 + c0 + n], gate_key="Z",
                                 res_fn=lambda j, m, c0, n: mr[j * 128:j * 128 + m, c0:c0 + n], res_key="mr")
            B.mm_phase(xt2, ["xt2"], c.DC // 128, main_tiles, proj_c_t, list(range(KT)), full, epi)
            B.load_xt(xt2, "xt2", mg, "mg", KT, T)
            epi = B.std_epilogue(tv(y1), F32, "y1", res_fn=tv(xTf), res_key="xTf_f32", res_scale=c.alpha, out_tiled=True, res_tiled=True)
            B.mm_phase(xt2, ["xt2"], KT, main_tiles, w_out_t, list(range(KT)), full, epi)
        with P.scope():
            B.alloc_ln()
            B.ln_phase(y1, "y1", KT, T, lnv[:, 0, :, :], 0, 1e-5, out_f32=h1, out_f32_key="h1", xt=xt2, xt_key="xt2", name="ln1")
        if debug == "p4":
            P.finish(["h1"]); return B
        with P.scope():
            B.alloc_mm()
            epi = B.std_epilogue(lambda j, m, c0, n: qT[j * 128:j * 128 + m, c0:c0 + n], BF16, "qT")
            B.mm_phase(xt2, ["xt2"], KT, main_tiles, wq_t, list(range(KT)), full, epi)
    with P.scope():
        kT = P.sbuf("kT", [128, KT, c.MEM], BF16)
        vtm = P.sbuf("vtm", [128, c.MEM // 128, D], BF16)
        with P.scope():
            mxt = P.sbuf("mxt", [128, KT, c.MEM], BF16)
            with P.scope():
                B.alloc_ln()
                B.ln_phase(memT, "memT_f32", KT, c.MEM, lnv[:, 3, :, :], 0, 1e-5, xt=mxt, xt_key="mxt", name="lnm")
            with P.scope():
                B.alloc_mm()
                cntk = [0]

                def epi_k(j, m, c0, n, pso, bk):
                    P.op("act", lambda e: e.activation(out=kT[:, j, c0:c0 + n], in_=pso, func=AF.Copy), reads=bk, writes=["kT"], merge=True)

                def epi_v(j, m, c0, n, pso, bk):
                    P.op("dve", lambda e: e.tensor_copy(out=vtm[0:n, c0 // 128, j * 128:j * 128 + m], in_=pso), reads=bk, writes=["vtm"], merge=True)
                B.mm_phase(mxt, ["mxt"], KT, [(0, c.MEM)], wk_t, list(range(KT)), full, epi_k)
                B.mm_phase(mxt, ["mxt"], KT, [(i * 128, 128) for i in range(c.MEM // 128)], wv_t, list(range(KT)), full, epi_v, swap=True)
        with P.scope():
            B._st16 = B.stager("st16a_", 2, [128, 512], BF16)
            B.attn_phase(qT, kT, vtm, oT)
    if debug == "p5":
        P.finish(["oT"]); return B
    with P.scope():
        xt2 = P.sbuf("xt2b", [128, KT, T], BF16)
        with P.scope():
            B.alloc_mm()
            B.load_xt(xt2, "xt2", oT, "oT", KT, T)
            epi = B.std_epilogue(tv(y2), F32, "y2", res_fn=tv(h1), res_key="h1", res_scale=c.alpha, out_tiled=True, res_tiled=True)
            B.mm_phase(xt2, ["xt2"], KT, main_tiles, wo_t, list(range(KT)), full, epi)
        with P.scope():
            B.alloc_ln()
            B.ln_phase(y2, "y2", KT, T, lnv[:, 1, :, :], 0, 1e-5, out_f32=h2, out_f32_key="h2", xt=xt2, xt_key="xt2", name="ln2")
        if debug == "p6":
            P.finish(["h2"]); return B
        with P.scope():
            B.alloc_mm()
            epi = B.std_epilogue(lambda j, m, c0, n: aT[j * 128:j * 128 + m, c0:c0 + n], BF16, "aT", act="relu2")
            B.mm_phase(xt2, ["xt2"], KT, main_tiles, w1_t, list(range(c.DFF // 128)), full, epi)
            for ch in range(c.DFF // D):
                B.load_xt(xt2, "xt2", aT[ch * D:(ch + 1) * D, :], "aT", KT, T)
                pkey = "h2" if ch == 0 else f"y3_{ch - 1}"
                lastc = (ch == c.DFF // D - 1)
                plain = lambda Yp: (lambda j, m, c0, n: Yp[j * 128:j * 128 + m, c0:c0 + n])
                epi = B.std_epilogue(tv(y3[ch]) if lastc else plain(y3[ch]), F32, f"y3_{ch}",
                                     res_fn=(tv(h2) if ch == 0 else plain(y3[ch - 1])), res_key=pkey,
                                     res_scale=(c.alpha if ch == 0 else 1.0), out_tiled=lastc, res_tiled=(ch == 0))
                B.mm_phase(xt2, ["xt2"], KT, main_tiles, w2_t[ch], list(range(KT)), full, epi)
    with P.scope():
        B.alloc_ln()
        B._xin = Ring("xino", [P.sbuf(f"xino{i}", [128, D], F32) for i in range(2)])
        last = c.DFF // D - 1
        B.ln_phase(y3[last], f"y3_{last}", KT, T, lnv[:, 2, :, :], 0, 1e-5, final_out=out, name="ln3")
    P.finish(["final"])
    return B


def make_consts():
    cst = np.zeros((128, 1792), np.float32)
    p = np.arange(128)
    h = p // 64; t = p % 64
    same = (h[:, None] == h[None, :]).astype(np.float32)
    cst[:, 0:128] = np.eye(128)
    cst[:, 128:256] = same
    cst[:, 256:384] = same * (t[None, :] < t[:, None])
    cst[:, 384:512] = same * (t[None, :] > t[:, None])
    cst[:, 512:640] = same * (t[None, :] >= t[:, None])
    cst[:, 640:704] = (t[:, None] == np.arange(64)[None, :])
    cst[:, 704] = (h == 0); cst[:, 705] = (h == 1)
    cst[:, 706] = -(h == 0).astype(np.float32); cst[:, 707] = -(h == 1).astype(np.float32)
    cst[:, 708:708 + 1024] = (np.arange(1024) % 64 != 0).astype(np.float32)[None, :]
    return cst


def tile_w(w, pad_to=None):
    K, N = w.shape
    NJ = (N + 127) // 128
    if NJ * 128 != N:
        w = np.concatenate([w, np.zeros((K, NJ * 128 - N), w.dtype)], axis=1)
    return np.ascontiguousarray(w.reshape(K // 128, 128, NJ, 128).transpose(2, 1, 0, 3))


def pvec(v):
    return np.ascontiguousarray(v.reshape(-1, 128).T)


def prepare_inputs(cfg, inputs):
    c = cfg
    f = lambda k: np.asarray(inputs[k], dtype=np.float32)
    sq = lambda k: f(k)[0]
    shared = {}
    shared["consts"] = make_consts()
    shared["w_in_t"] = tile_w(sq("w_in"))
    shared["proj_r_t"] = tile_w(sq("proj_rwkv")); shared["proj_c_t"] = tile_w(sq("proj_conv"))
    shared["w_out_t"] = tile_w(sq("w_out")); shared["wq_t"] = tile_w(sq("xattn_wq")); shared["wk_t"] = tile_w(sq("xattn_wk"))
    shared["wv_t"] = tile_w(sq("xattn_wv")); shared["wo_t"] = tile_w(sq("xattn_wo"))
    shared["w1_t"] = tile_w(sq("mlp_w1"))
    w2 = sq("mlp_w2")
    shared["w2_t"] = np.stack([tile_w(w2[i * c.D:(i + 1) * c.D]) for i in range(c.DFF // c.D)])
    lnv = np.zeros((128, 4, 2, c.KT), np.float32)
    for i, (g, b) in enumerate([("ln1_g", "ln1_b"), ("ln2_g", "ln2_b"), ("ln3_g", "ln3_b")]):
        lnv[:, i, 0] = pvec(sq(g)); lnv[:, i, 1] = pvec(sq(b))
    lnv[:, 3, 0] = pvec(f("ln_mem_g")); lnv[:, 3, 1] = pvec(f("ln_mem_b"))
    shared["lnv"] = lnv
    NCT = c.DC // 128
    cw = sq("conv_w")[:, 0, :]
    convv = np.zeros((128, NCT, 32), np.float32)
    convv[:, :, 0:31] = cw.T.reshape(NCT, 128, 31).transpose(1, 0, 2)
    convv[:, :, 31] = pvec(sq("conv_b"))
    shared["convv"] = convv
    clnv = np.zeros((128, 2, NCT), np.float32)
    clnv[:, 0] = pvec(sq("conv_ln_g")); clnv[:, 1] = pvec(sq("conv_ln_b"))
    shared["clnv"] = clnv
    NHP = c.NHP
    sm = sq("rwkv_shift_mix")
    rvec = np.zeros((128, 10 * NHP + 4), np.float32)
    cols = [sm[0:c.DR], sm[c.DR:2 * c.DR], sm[2 * c.DR:3 * c.DR], sq("rwkv_w0"), sq("rwkv_a0"), sq("rwkv_k_k"), sq("rwkv_k_a"),
            sq("rwkv_r_k").reshape(-1), sq("rwkv_gn_g"), sq("rwkv_gn_b")]
    for i, v in enumerate(cols):
        rvec[:, i * NHP:(i + 1) * NHP] = pvec(v)
    rvec[0:96, 10 * NHP] = sm[c.C_DW:c.C_DW + 96]; rvec[0:96, 10 * NHP + 1] = sm[c.C_DA:c.C_DA + 96]
    rvec[:, 10 * NHP + 2:10 * NHP + 4] = pvec(sm[c.C_DG:c.C_DG + 256])
    shared["rvec"] = rvec
    wu = np.zeros((128, c.DR), np.float32); wu[0:96] = sq("rwkv_w_up"); shared["w_up_t"] = wu
    au = np.zeros((128, c.DR), np.float32); au[0:96] = sq("rwkv_a_up"); shared["a_up_t"] = au
    shared["g_up_t"] = np.ascontiguousarray(sq("rwkv_g_up").reshape(2, 128, c.DR).transpose(1, 0, 2))
    x = f("x"); mem = f("mem")
    in_maps = []
    for core in range(c.NCORES):
        b, sh = core // 2, core % 2
        m = dict(shared)
        m["x_main"] = np.ascontiguousarray(x[b, sh * c.T:(sh + 1) * c.T])
        m["x_prev"] = np.ascontiguousarray(x[b, 0:c.T]) if sh == 1 else np.zeros((c.T, c.D), np.float32)
        m["mem_b"] = np.ascontiguousarray(mem[b])
        in_maps.append(m)
    return in_maps


_CACHE = {}


def kernel(**inputs):
    cfg = Cfg()
    if "B" not in _CACHE:
        _CACHE["B"] = build(cfg)
    B = _CACHE["B"]
    in_maps = prepare_inputs(cfg, inputs)
    res = run_bass_kernel_spmd(B.nc, in_maps, core_ids=list(range(cfg.NCORES)))
    out = np.zeros((cfg.BATCH, cfg.SEQ, cfg.D), np.float32)
    for core in range(cfg.NCORES):
        b, sh = core // 2, core % 2
        out[b, sh * cfg.T:(sh + 1) * cfg.T] = res.results[core]["out"]
    return out
```
